# Optimizing a Trainium2 kernel written in Bass

```python
import jax, jax.numpy as jnp
from jax import lax
import numpy as np

D_MODEL = 2048
BATCH = 2
SEQ = 16384
DEPTH = 2
DEC_BATCH = 32
DEC_SEQ = 64
PAST_LEN = 2048

CHUNK = 64
W_A = D_MODEL
N_LRU_HEADS = 16
LRU_HEAD_DIM = W_A // N_LRU_HEADS
CONV_WIDTH = 4
LRU_C = 8.0
W_B = D_MODEL
POOL_WINDOWS = (2, 4, 8, 16)
N_POOL_GROUPS = 4
POOL_GROUP_DIM = W_B // N_POOL_GROUPS
POOL_PAD = POOL_WINDOWS[-1] - 1
D_IN = 2 * W_A + 2 * W_B + 2 * D_MODEL
EPS = 1e-6

kernel_name = "hybrid_rglru_pool_stream_step"


def rms_norm(x, g):
    xf = x.astype(jnp.float32)
    y = xf * lax.rsqrt(jnp.mean(xf * xf, axis=-1, keepdims=True) + EPS)
    return (y * g.astype(jnp.float32)).astype(x.dtype)


def causal_dwconv(xa, buf, w, b):
    T = xa.shape[1]
    xp = jnp.concatenate([buf.astype(xa.dtype), xa], axis=1)
    y = b
    for k in range(CONV_WIDTH):
        y = y + xp[:, k:k + T] * w[k]
    return y, xp[:, -(CONV_WIDTH - 1):]


def block_diag(x, w, b):
    bsz, T, _ = x.shape
    xh = x.reshape(bsz, T, N_LRU_HEADS, LRU_HEAD_DIM)
    return jnp.einsum('bthd,hde->bthe', xh, w).reshape(bsz, T, W_A) + b


def rg_lru(x, h0, wa, ba, wx, bx, lam):
    r = jax.nn.sigmoid(block_diag(x, wa, ba).astype(jnp.float32))
    i = jax.nn.sigmoid(block_diag(x, wx, bx).astype(jnp.float32))
    log_a = -LRU_C * r * jax.nn.softplus(-lam.astype(jnp.float32))
    a = jnp.exp(log_a)
    mult = jnp.sqrt(jnp.maximum(-jnp.expm1(2.0 * log_a), 0.0))
    u = mult * i * x.astype(jnp.float32)

    def combine(left, right):
        a1, b1 = left
        a2, b2 = right
        return a1 * a2, a2 * b1 + b2

    a_cum, b_cum = lax.associative_scan(combine, (a, u), axis=1)
    h = a_cum * h0.astype(jnp.float32)[:, None, :] + b_cum
    return h, h[:, -1]


def pool_mix(xb, buf, p0, pool_w, pool_scale):
    bsz, T, _ = xb.shape
    xp = jnp.concatenate([buf.astype(xb.dtype), xb], axis=1).astype(jnp.float32)
    S = jnp.concatenate([jnp.zeros((bsz, 1, W_B), jnp.float32), jnp.cumsum(xp, axis=1)], axis=1)
    pos = p0 + jnp.arange(T)
    outs = []
    for g, w in enumerate(POOL_WINDOWS):
        sl = slice(g * POOL_GROUP_DIM, (g + 1) * POOL_GROUP_DIM)
        win = S[:, POOL_PAD + 1:POOL_PAD + 1 + T, sl] - S[:, POOL_PAD + 1 - w:POOL_PAD + 1 - w + T, sl]
        cnt = jnp.minimum(w, pos + 1).astype(jnp.float32)[None, :, None]
        outs.append(win / cnt - xp[:, POOL_PAD:, sl])
    d = jnp.stack(outs, axis=2).astype(xb.dtype)
    y = jnp.einsum('btgc,gce->btge', d, pool_w).reshape(bsz, T, W_B)
    return y * pool_scale


def layer(x, c, conv_buf, h0, pool_buf, p0, norm_g, w_ada, b_ada, w_in, conv_w, conv_b,
          lru_wa, lru_ba, lru_wx, lru_bx, lru_lam, pool_w, pool_scale, w_proj_a, w_proj_b, w_out):
    mod = jax.nn.silu(c) @ w_ada + b_ada
    shift, scale, gate = jnp.split(mod, 3, axis=-1)
    h = rms_norm(x, norm_g) * (1.0 + scale[:, None]) + shift[:, None]
    z = h @ w_in
    xa, ga, xb, gb, ma, mb = jnp.split(
        z, [W_A, 2 * W_A, 2 * W_A + W_B, 2 * W_A + 2 * W_B, 2 * W_A + 2 * W_B + D_MODEL], axis=-1)
    xa_c, conv_new = causal_dwconv(xa, conv_buf, conv_w, conv_b)
    hs, h_last = rg_lru(xa_c, h0, lru_wa, lru_ba, lru_wx, lru_bx, lru_lam)
    ya = hs.astype(x.dtype) * jax.nn.silu(ga)
    yb = pool_mix(xb, pool_buf, p0, pool_w, pool_scale) * jax.nn.silu(gb)
    pool_new = jnp.concatenate([pool_buf.astype(xb.dtype), xb], axis=1)[:, -POOL_PAD:]
    m = jax.nn.sigmoid(ma) * (ya @ w_proj_a) + jax.nn.sigmoid(mb) * (yb @ w_proj_b)
    x = x + gate[:, None] * (m @ w_out)
    return x, conv_new, h_last.astype(x.dtype), pool_new


def setup_inputs(seed: int = 0) -> dict:
    key = jax.random.key(seed)
    ks = jax.random.split(key, 32)
    f32 = jnp.float32
    nrm = lambda k, s, sc: jax.random.normal(k, s, f32) * sc
    a0 = jax.random.uniform(ks[12], (DEPTH, W_A), f32, 0.9, 0.999)
    sig = a0 ** (1.0 / LRU_C)
    lru_lam = jnp.log(sig / (1.0 - sig))
    return {
        "x_prompt": nrm(ks[0], (BATCH, SEQ, D_MODEL), 1.0),
        "x_sample": nrm(ks[1], (DEC_BATCH, DEC_SEQ, D_MODEL), 1.0),
        "c_prompt": nrm(ks[2], (BATCH, D_MODEL), 1.0),
        "c_sample": nrm(ks[3], (DEC_BATCH, D_MODEL), 1.0),
        "state_conv": nrm(ks[4], (DEPTH, DEC_BATCH, CONV_WIDTH - 1, W_A), 1.0),
        "state_lru": nrm(ks[5], (DEPTH, DEC_BATCH, W_A), 0.5),
        "state_pool": nrm(ks[6], (DEPTH, DEC_BATCH, POOL_PAD, W_B), 1.0),
        "norm_g": 1.0 + nrm(ks[7], (DEPTH, D_MODEL), 0.05),
        "w_ada": nrm(ks[8], (DEPTH, D_MODEL, 3 * D_MODEL), D_MODEL ** -0.5),
        "b_ada": nrm(ks[9], (DEPTH, 3 * D_MODEL), 0.01),
        "w_in": nrm(ks[10], (DEPTH, D_MODEL, D_IN), D_MODEL ** -0.5),
        "conv_w": nrm(ks[11], (DEPTH, CONV_WIDTH, W_A), CONV_WIDTH ** -0.5),
        "conv_b": nrm(ks[13], (DEPTH, W_A), 0.01),
        "lru_wa": nrm(ks[14], (DEPTH, N_LRU_HEADS, LRU_HEAD_DIM, LRU_HEAD_DIM), LRU_HEAD_DIM ** -0.5),
        "lru_ba": nrm(ks[15], (DEPTH, W_A), 0.01),
        "lru_wx": nrm(ks[16], (DEPTH, N_LRU_HEADS, LRU_HEAD_DIM, LRU_HEAD_DIM), LRU_HEAD_DIM ** -0.5),
        "lru_bx": nrm(ks[17], (DEPTH, W_A), 0.01),
        "lru_lam": lru_lam,
        "pool_w": nrm(ks[18], (DEPTH, N_POOL_GROUPS, POOL_GROUP_DIM, POOL_GROUP_DIM), POOL_GROUP_DIM ** -0.5),
        "pool_scale": 1.0 + nrm(ks[19], (DEPTH, W_B), 0.1),
        "w_proj_a": nrm(ks[20], (DEPTH, W_A, D_MODEL), W_A ** -0.5),
        "w_proj_b": nrm(ks[21], (DEPTH, W_B, D_MODEL), W_B ** -0.5),
        "w_out": nrm(ks[22], (DEPTH, D_MODEL, D_MODEL), D_MODEL ** -0.5),
        "final_g": 1.0 + nrm(ks[23], (D_MODEL,), 0.05),
    }


def reference(x_prompt, x_sample, c_prompt, c_sample, state_conv, state_lru, state_pool,
              norm_g, w_ada, b_ada, w_in, conv_w, conv_b, lru_wa, lru_ba, lru_wx, lru_bx,
              lru_lam, pool_w, pool_scale, w_proj_a, w_proj_b, w_out, final_g):
    xp, xs = x_prompt, x_sample
    bp = x_prompt.shape[0]
    conv_p, lru_p, pool_p = [], [], []
    conv_s, lru_s, pool_s = [], [], []
    for l in range(DEPTH):
        wl = (norm_g[l], w_ada[l], b_ada[l], w_in[l], conv_w[l], conv_b[l], lru_wa[l], lru_ba[l],
              lru_wx[l], lru_bx[l], lru_lam[l], pool_w[l], pool_scale[l], w_proj_a[l], w_proj_b[l], w_out[l])
        xp, cb, hl, pb = layer(xp, c_prompt,
                               jnp.zeros((bp, CONV_WIDTH - 1, W_A), xp.dtype),
                               jnp.zeros((bp, W_A), jnp.float32),
                               jnp.zeros((bp, POOL_PAD, W_B), xp.dtype),
                               0, *wl)
        conv_p.append(cb); lru_p.append(hl); pool_p.append(pb)
        xs, cb, hl, pb = layer(xs, c_sample, state_conv[l], state_lru[l], state_pool[l],
                               PAST_LEN, *wl)
        conv_s.append(cb); lru_s.append(hl); pool_s.append(pb)
    y_prompt = rms_norm(xp, final_g)
    y_sample = rms_norm(xs, final_g)
    return (y_prompt, y_sample,
            jnp.stack(conv_p), jnp.stack(lru_p), jnp.stack(pool_p),
            jnp.stack(conv_s), jnp.stack(lru_s), jnp.stack(pool_s))
```

```python
from contextlib import ExitStack
import numpy as np
import concourse.bass as bass
import concourse.mybir as mybir
from concourse.bass_utils import run_bass_kernel_spmd

F32 = mybir.dt.float32
BF16 = mybir.dt.bfloat16
AF = mybir.ActivationFunctionType
ALU = mybir.AluOpType

D = 2048
NCH = 16
SEQ = 16384
NT_P = SEQ // 512
DEPTH = 2
NCORES = 8
EPS = 1e-6
SEM_LIMIT = 30000
DEBUG_STOP = 0


class StopBuild(Exception):
    pass


class Op:
    __slots__ = ("eng", "fn", "waits", "signal", "sem", "cnt", "dma", "inc")

    def __init__(self, eng, fn, dma):
        self.eng = eng
        self.fn = fn
        self.waits = []
        self.signal = False
        self.sem = None
        self.cnt = 0
        self.dma = dma
        self.inc = 0


class Sched:
    ENGS = ("pe", "act", "dve", "pool", "sp")

    def __init__(self):
        self.ops = {e: [] for e in self.ENGS}
        self.lastw = {}
        self.readers = {}
        self.dmacount = {}

    def add(self, eng, fn, reads=(), writes=(), dma=None):
        op = Op(eng, fn, dma)
        deps = []
        for k in reads:
            w = self.lastw.get(k)
            if w is not None:
                deps.append((w, True))
            if isinstance(k, tuple) and k[0] == "ps":
                for r in self.readers.get(k, ()):
                    if r.eng != eng:
                        deps.append((r, False))
        for k in writes:
            w = self.lastw.get(k)
            if w is not None:
                deps.append((w, False))
            for r in self.readers.get(k, ()):
                deps.append((r, False))
        for (y, raw) in deps:
            if y is op:
                continue
            if y.dma is not None:
                op.waits.append(("dma", y.dma, 16 * self.dmacount[y.dma]))
            else:
                if y.eng == eng and eng == "pe":
                    continue
                y.signal = True
                op.waits.append(("eng", y))
        for k in writes:
            self.lastw[k] = op
            self.readers[k] = []
        for k in reads:
            self.readers.setdefault(k, []).append(op)
        if dma is not None:
            self.dmacount[dma] = self.dmacount.get(dma, 0) + 1
        self.ops[eng].append(op)
        return op

    def fence(self):
        lasts = []
        for e in self.ENGS:
            real = [o for o in self.ops[e] if o.fn is not None and o.dma is None]
            if real:
                lasts.append(real[-1])
        dm = dict(self.dmacount)
        for e in self.ENGS:
            op = Op(e, None, None)
            for y in lasts:
                if y.dma is None and y.eng != e:
                    y.signal = True
                    op.waits.append(("eng", y))
            for k, c in dm.items():
                op.waits.append(("dma", k, 16 * c))
            self.ops[e].append(op)
        self.lastw = {}
        self.readers = {}

    def final_wait(self):
        op = Op("sp", None, None)
        for k, c in self.dmacount.items():
            op.waits.append(("dma", k, 16 * c))
        for e in self.ENGS:
            if e == "sp":
                continue
            real = [o for o in self.ops[e] if o.fn is not None and o.dma is None]
            if real:
                real[-1].signal = True
                op.waits.append(("eng", real[-1]))
        self.ops["sp"].append(op)

    def emit(self, nc, es):
        engsems = {}
        for e in self.ENGS:
            cur = None
            cnt = 0
            n = 0
            for op in self.ops[e]:
                if op.dma is not None or not op.signal:
                    continue
                if cur is None or cnt >= SEM_LIMIT:
                    cur = es.enter_context(nc.semaphore(f"s_{e}_{n}"))
                    n += 1
                    cnt = 0
                cnt += 1
                op.sem = cur
                op.cnt = cnt
        dmasems = {k: es.enter_context(nc.semaphore(f"d_{i}")) for i, k in enumerate(self.dmacount)}
        block = es.enter_context(nc.Block())

        def run(ename, eng):
            waited = {}
            for op in self.ops[ename]:
                for w in op.waits:
                    if w[0] == "dma":
                        sem, val = dmasems[w[1]], w[2]
                    else:
                        sem, val = w[1].sem, w[1].cnt
                    key = id(sem)
                    if waited.get(key, 0) >= val:
                        continue
                    waited[key] = val
                    eng.wait_ge(sem, val)
                if op.fn is None:
                    continue
                inst = op.fn(eng)
                if op.dma is not None:
                    inst.then_inc(dmasems[op.dma], 16)
                elif op.signal:
                    inst.then_inc(op.sem, 1)

        @block.tensor
        def _(eng):
            run("pe", eng)

        @block.scalar
        def _(eng):
            run("act", eng)

        @block.vector
        def _(eng):
            run("dve", eng)

        @block.gpsimd
        def _(eng):
            run("pool", eng)

        @block.sync
        def _(eng):
            run("sp", eng)


def build_program():
    nc = bass.Bass("TRN2", target_bir_lowering=False)
    S = Sched()
    es = ExitStack()

    def din(name, shape, dt=F32):
        return nc.dram_tensor(name, list(shape), dt, kind="ExternalInput").ap()

    def dout(name, shape, dt=F32):
        return nc.dram_tensor(name, list(shape), dt, kind="ExternalOutput").ap()

    def dint(name, shape, dt):
        return nc.dram_tensor(name, list(shape), dt, kind="Internal").ap()

    xp = din("xp", [SEQ, D])
    xs = din("xs", [256, D])
    c5T_d = din("c5T", [128, NCH * 5])
    vecs_d = din("vecs", [128, 2 * 10 * NCH])
    badaT_d = din("badaT", [128, 2 * 32])
    bgate_d = din("bgate", [5, 2 * D])
    sconv_d = din("sconvT", [128, 2 * 4 * NCH * 3])
    spool_d = din("spoolT", [128, 2 * 4 * NCH * 15])
    slru_d = din("slruT", [128, 2 * 4 * NCH])
    fgb_d = din("fgb", [128, D])
    sel_d = din("sel", [5, 3 * 128])
    invc_d = din("invc", [128, 4 * 16])
    ident_d = din("ident", [128, 128])
    w_ada = din("w_ada", [2, D, 3 * D])
    w_in = din("w_in", [2, D, 6 * D])
    lru_wa = din("lru_wa", [2, D, 128])
    lru_wx = din("lru_wx", [2, D, 128])
    pool_w = din("pool_w", [2, D, 512])
    w_pa = din("w_proj_a", [2, D, D])
    w_pb = din("w_proj_b", [2, D, D])
    w_out = din("w_out", [2, D, D])

    yp = dout("yp", [SEQ, D])
    ys = dout("ys", [256, D])
    oconv_d = dout("o_conv", [128, 2 * 5 * NCH * 3])
    opool_d = dout("o_pool", [128, 2 * 5 * NCH * 15])
    olru_d = dout("o_lru", [128, 2 * 5 * NCH])

    win_s = dint("win_s", [2, 96, 128, 2048], BF16)
    pa_s = dint("pa_s", [2, 16, 128, 2048], BF16)
    pb_s = dint("pb_s", [2, 16, 128, 2048], BF16)
    wo_s = dint("wo_s", [2, 4, 128, 8192], BF16)
    pw_s = dint("pw_s", [2, 128, 8192], BF16)
    la_s = dint("la_s", [2, 128, 2048], BF16)
    lx_s = dint("lx_s", [2, 128, 2048], BF16)
    gbc_s = dint("gbc_s", [2, 3, 128, D], F32)

    def sb(name, shape, dt=F32):
        return es.enter_context(nc.sbuf_tensor(name, list(shape), dt))

    NS = 8
    xt = sb("xt", [128, 4, D])
    ring = sb("ring", [128, NS, 2048], BF16)
    hT = sb("hT", [128, NCH, 512], BF16)
    ya = sb("ya", [128, NCH, 512], BF16)
    yb = sb("yb", [128, NCH, 512], BF16)
    mT = sb("mT", [128, NCH, 512], BF16)
    lwa = sb("lwa", [128, NCH, 128], BF16)
    lwx = sb("lwx", [128, NCH, 128], BF16)
    xn1 = sb("xn", [128, D], BF16)
    xn = [xn1, xn1]
    bc = sb("bc", [128, D])
    dbf = sb("dbf", [128, 4, 512], BF16)
    sgb = sb("sgb", [128, 4, 512])
    tmpn = ["tr", "ti", "a2", "rs", "h", "tgb"]
    tmp = {n: [sb(f"{n}", [128, 512])] * 2 for n in tmpn}
    tmp["cv"] = [sb(f"cv{i}", [128, 512]) for i in range(2)]
    tmp["tg"] = [sb(f"tg{i}", [128, 512]) for i in range(2)]
    q25 = sb("q25", [128, 1])
    cvb = [sb(f"cvb{i}", [128, 512], BF16) for i in range(2)]
    ext1 = sb("ext", [128, 528])
    ext = [ext1, ext1]
    pext1 = sb("pext", [128, 528])
    pext = [pext1, pext1]
    pA = sb("pA", [128, 528])
    pB = sb("pB", [128, 528])
    identb = sb("identb", [128, 128], BF16)
    vecs = sb("vecs_t", [128, 2, 10, NCH])
    bah = sb("bah", [128, 2, NCH])
    bxh = sb("bxh", [128, 2, NCH])
    chh = sb("chh", [128, 2, NCH])
    psh = sb("psh", [128, 2, NCH])
    sc1 = sb("sc1", [128, 2, NCH, 5])
    shf = sb("shf", [128, 2, NCH, 5])
    Sconv = sb("Sconv", [128, 2, 5, NCH, 3])
    Spool = sb("Spool", [128, 2, 5, NCH, 15])
    Slru = sb("Slru", [128, 2, 5, NCH])
    invc = sb("invc_t", [128, 4, 16])
    ss = sb("ss", [128, 4])
    sd = sb("sd", [128, 4])
    rstd = sb("rstd", [128, 4])
    small = sb("small", [128, 64])
    c5T = tmp["tg"][0][:, 0:80].rearrange("p (a b) -> p a b", a=NCH)
    scT = tmp["tg"][0][:, 80:160].rearrange("p (a b) -> p a b", a=NCH)
    badaT = tmp["tg"][0][:, 160:224].rearrange("p (a b) -> p a b", a=2)
    modT = tmp["tg"][0][:, 224:384].rearrange("p (a b) -> p a b", a=32)
    bgate = yb[:].rearrange("p a b -> p (a b)").bitcast(F32)[0:5, :].rearrange("p (a b) -> p a b", a=2)
    grow = mT[:].rearrange("p a b -> p (a b)").bitcast(F32)[0:5, 0:D]
    sel = tmp["tg"][1][0:5, 0:384].rearrange("p (a b) -> p a b", a=3)
    identf = tmp["cv"][0][:, 0:128]

    ps = [es.enter_context(nc.psum_tensor(f"ps{i}", [128, 512], F32)) for i in range(8)]

    def PS(i):
        return ("ps", i)

    def dma(eng, key, out, in_, reads=(), writes=(), **kw):
        S.add(eng, lambda e: e.dma_start(out=out, in_=in_, **kw), reads=reads, writes=writes, dma=key)

    dma("sp", "c0", vecs[:].rearrange("p a b c -> p (a b c)"), vecs_d, writes=["vecs"])
    dma("sp", "c0", c5T[:].rearrange("p a b -> p (a b)"), c5T_d, writes=["c5T"])
    dma("sp", "c0", badaT[:].rearrange("p a b -> p (a b)"), badaT_d, writes=["badaT"])
    dma("sp", "c0", yb[:].rearrange("p a b -> p (a b)").bitcast(F32)[0:5, :], bgate_d, writes=["bgate"])
    dma("sp", "c0", sel[:].rearrange("p a b -> p (a b)"), sel_d, writes=["sel"])
    dma("sp", "c0", invc[:].rearrange("p a b -> p (a b)"), invc_d, writes=["invc"])
    dma("sp", "c0", identf[:], ident_d, writes=["identf"])
    S.add("dve", lambda e: e.memset(Sconv[:].rearrange("p a b c d -> p (a b c d)"), 0.0), writes=["Sconv0"])
    S.add("dve", lambda e: e.memset(Spool[:].rearrange("p a b c d -> p (a b c d)"), 0.0), writes=["Spool0"])
    S.add("dve", lambda e: e.memset(Slru[:].rearrange("p a b c -> p (a b c)"), 0.0), writes=["Slru0"])
    for l in range(2):
        dma("sp", "c0", Sconv[:, l, 1:5].rearrange("p s c r -> p (s c r)"),
            sconv_d[:, l * 192:(l + 1) * 192], reads=["Sconv0"], writes=[("Sconv", l)])
        dma("sp", "c0", Spool[:, l, 1:5].rearrange("p s c r -> p (s c r)"),
            spool_d[:, l * 960:(l + 1) * 960], reads=["Spool0"], writes=[("Spool", l)])
        dma("sp", "c0", Slru[:, l, 1:5].rearrange("p s c -> p (s c)"),
            slru_d[:, l * 64:(l + 1) * 64], reads=["Slru0"], writes=[("Slru", l)])
    S.add("dve", lambda e: e.tensor_copy(out=identb[:], in_=identf[:]), reads=["identf"], writes=["identb"])
    for l in range(2):
        S.add("dve", lambda e, l=l: e.tensor_scalar(out=bah[:, l, :], in0=vecs[:, l, 6, :], scalar1=0.5,
                                                     scalar2=None, op0=ALU.mult), reads=["vecs"], writes=["bah"])
        S.add("dve", lambda e, l=l: e.tensor_scalar(out=bxh[:, l, :], in0=vecs[:, l, 7, :], scalar1=0.5,
                                                     scalar2=None, op0=ALU.mult), reads=["vecs"], writes=["bxh"])
        S.add("dve", lambda e, l=l: e.tensor_scalar(out=psh[:, l, :], in0=vecs[:, l, 9, :], scalar1=0.5,
                                                     scalar2=None, op0=ALU.mult), reads=["vecs"], writes=["psh"])
        S.add("act", lambda e, l=l: e.activation(out=small[:, l * 16:(l + 1) * 16], in_=vecs[:, l, 8, :],
                                                  func=AF.Exp, scale=-1.0), reads=["vecs"], writes=[("small", l)])
        S.add("act", lambda e, l=l: e.activation(out=small[:, 32 + l * 16:32 + (l + 1) * 16],
                                                  in_=small[:, l * 16:(l + 1) * 16], func=AF.Ln, bias=1.0),
              reads=[("small", l)], writes=[("small2", l)])
        S.add("dve", lambda e, l=l: e.tensor_scalar(out=chh[:, l, :], in0=small[:, 32 + l * 16:32 + (l + 1) * 16],
                                                     scalar1=-4.0, scalar2=None, op0=ALU.mult),
              reads=[("small2", l)], writes=["chh"])
    S.add("act", lambda e: e.activation(out=scT[:].rearrange("p a b -> p (a b)"),
                                        in_=c5T[:].rearrange("p a b -> p (a b)"), func=AF.Tanh, scale=0.5),
          reads=["c5T"], writes=["scT0"])
    S.add("dve", lambda e: e.scalar_tensor_tensor(out=scT[:].rearrange("p a b -> p (a b)"),
                                                  in0=scT[:].rearrange("p a b -> p (a b)"), scalar=1.0,
                                                  in1=c5T[:].rearrange("p a b -> p (a b)"),
                                                  op0=ALU.add, op1=ALU.mult),
          reads=["scT0", "c5T"], writes=["scT1"])
    S.add("dve", lambda e: e.tensor_scalar(out=scT[:].rearrange("p a b -> p (a b)"),
                                           in0=scT[:].rearrange("p a b -> p (a b)"), scalar1=0.5, scalar2=None,
                                           op0=ALU.mult), reads=["scT1"], writes=["scT"])

    if DEBUG_STOP == 10:
        S.final_wait()
        S.emit(nc, es)
        es.close()
        return nc
    stf = [xt[:].rearrange("p a b -> p (a b)"), ring[:].rearrange("p a b -> p (a b)").bitcast(F32)[:, 0:8192]]
    stb = [hT[:].rearrange("p a b -> p (a b)"), ya[:].rearrange("p a b -> p (a b)")]
    cvt_i = [0]
    cast_engs = ["act", "dve", "pool"]

    def convert(src_ap_3d, dst_ap, relayout4):
        i = cvt_i[0]
        cvt_i[0] += 1
        b = i % 2
        f = stf[b]
        o = stb[b]
        dma("sp", ("cvl", b), f.rearrange("p (k n) -> p k n", k=16), src_ap_3d,
            writes=[("stf", b)])
        eng = cast_engs[i % 3]
        if relayout4:
            oin = f.rearrange("p (k i j) -> p i k j", k=16, i=4)
            oout = o.rearrange("p (i k j) -> p i k j", i=4, k=16)
        else:
            oin = f
            oout = o
        if eng == "act":
            S.add("act", lambda e: e.activation(out=oout, in_=oin, func=AF.Copy),
                  reads=[("stf", b)], writes=[("stb", b)])
        else:
            S.add(eng, lambda e: e.tensor_copy(out=oout, in_=oin), reads=[("stf", b)], writes=[("stb", b)])
        dma("sp", ("cvs", b), dst_ap, (o.rearrange("p (i f) -> p i f", i=4) if relayout4 else o), reads=[("stb", b)])

    for l in range(2):
        for blk in range(24):
            convert(w_in[l][:, blk * 512:(blk + 1) * 512].rearrange("(k p) n -> p k n", p=128),
                    win_s[l, blk * 4:(blk + 1) * 4].rearrange("i p f -> p i f"), True)
        for (src, dst) in ((w_pa, pa_s), (w_pb, pb_s)):
            for blk in range(4):
                convert(src[l][:, blk * 512:(blk + 1) * 512].rearrange("(k p) n -> p k n", p=128),
                        dst[l, blk * 4:(blk + 1) * 4].rearrange("i p f -> p i f"), True)
        for nb in range(4):
            convert(w_out[l][:, nb * 512:(nb + 1) * 512].rearrange("(k p) n -> p k n", p=128),
                    wo_s[l, nb], False)
        convert(pool_w[l].rearrange("(k p) n -> p k n", p=128), pw_s[l], False)
    for l in range(2):
        for (src, dst) in ((lru_wa, la_s), (lru_wx, lx_s)):
            i = cvt_i[0]
            cvt_i[0] += 1
            b = i % 2
            f = stf[b][:, 0:2048]
            o = stb[b][:, 0:2048]
            dma("sp", ("cvl", b), f.rearrange("p (k n) -> p k n", k=16),
                src[l].rearrange("(k p) n -> p k n", p=128), writes=[("stf", b)])
            S.add("dve", lambda e, f=f, o=o: e.tensor_copy(out=o, in_=f), reads=[("stf", b)], writes=[("stb", b)])
            dma("sp", ("cvs", b), dst[l], o, reads=[("stb", b)])

    if DEBUG_STOP == 11:
        S.final_wait()
        S.emit(nc, es)
        es.close()
        return nc
    for l in range(2):
        for blk in range(12):
            i = cvt_i[0]
            cvt_i[0] += 1
            b = i % 2
            f3 = stf[b].rearrange("p (k n) -> p k n", k=16)
            dma("sp", ("cvl", b), f3, w_ada[l][:, blk * 512:(blk + 1) * 512].rearrange("(k p) n -> p k n", p=128),
                writes=[("stf", b)])
            if blk < 8:
                for j in range(4):
                    cg = blk * 4 + j
                    for k in range(16):
                        S.add("pe", lambda e, f3=f3, j=j, k=k: e.matmul(
                            ps[0][:, j * 8:j * 8 + 5], lhsT=f3[:, k, j * 128:(j + 1) * 128], rhs=scT[:, k, :],
                            start=(k == 0), stop=(k == 15)),
                            reads=[("stf", b), "scT"], writes=[PS(0)])
                    S.add("dve", lambda e, j=j, cg=cg, l=l: e.tensor_scalar(
                        out=modT[:, cg, :], in0=ps[0][:, j * 8:j * 8 + 5], scalar1=badaT[:, l, cg:cg + 1],
                        scalar2=None, op0=ALU.add), reads=[PS(0), "badaT"], writes=[("modT", cg)])
                    if cg < 16:
                        S.add("dve", lambda e, cg=cg, l=l: e.tensor_copy(out=shf[:, l, cg, :], in_=modT[:, cg, :]),
                              reads=[("modT", cg)], writes=["shf"])
                    else:
                        c = cg - 16
                        S.add("dve", lambda e, cg=cg, c=c, l=l: e.tensor_scalar(
                            out=sc1[:, l, c, :], in0=modT[:, cg, :], scalar1=1.0, scalar2=vecs[:, l, 0, c:c + 1],
                            op0=ALU.add, op1=ALU.mult), reads=[("modT", cg), "vecs"], writes=["sc1"])
            else:
                nb = blk - 8
                for k in range(16):
                    S.add("pe", lambda e, f3=f3, k=k: e.matmul(
                        ps[1][0:5, :], lhsT=scT[:, k, :], rhs=f3[:, k, :], start=(k == 0), stop=(k == 15)),
                        reads=[("stf", b), "scT"], writes=[PS(1)])
                S.add("dve", lambda e, nb=nb, l=l: e.tensor_tensor(
                    out=grow[0:5, nb * 512:(nb + 1) * 512], in0=ps[1][0:5, :],
                    in1=bgate[:, l, nb * 512:(nb + 1) * 512], op=ALU.add),
                    reads=[PS(1), "bgate"], writes=[("grow", nb)])
        for cfg in range(3):
            o = stb[cfg % 2][:, 0:4096].bitcast(F32)
            for nb in range(4):
                S.add("pe", lambda e, cfg=cfg, nb=nb: e.matmul(
                    ps[2 + nb][:, :], lhsT=sel[:, cfg, :], rhs=grow[0:5, nb * 512:(nb + 1) * 512],
                    start=True, stop=True), reads=["sel", ("grow", nb)], writes=[PS(2 + nb)])
                S.add("act", lambda e, o=o, nb=nb: e.activation(
                    out=o[:, nb * 512:(nb + 1) * 512], in_=ps[2 + nb][:, :], func=AF.Copy, scale=0.5),
                    reads=[PS(2 + nb)], writes=[("stb", cfg % 2)])
            dma("sp", ("cvs", cfg % 2), gbc_s[l, cfg], o, reads=[("stb", cfg % 2)])

    if DEBUG_STOP == 12:
        S.final_wait()
        S.emit(nc, es)
        es.close()
        return nc
    S.fence()
    if DEBUG_STOP == 13:
        S.final_wait()
        S.emit(nc, es)
        es.close()
        return nc

    ring_pos = [0]

    def ring_load(src_ap, nslots=1):
        p = ring_pos[0]
        if nslots == 4:
            p = ((p + 3) // 4) * 4
        if p + nslots > NS:
            p = 0
        ring_pos[0] = (p + nslots) % NS
        keys = [("ring", p + i) for i in range(nslots)]
        out = ring[:, p:p + nslots, :].rearrange("p a b -> p (a b)")
        dma("sp", ("ring", p), out, src_ap, writes=keys)
        return p, keys

    xstage = [hT[:].rearrange("p a b -> p (a b)").bitcast(F32), ya[:].rearrange("p a b -> p (a b)").bitcast(F32)]
    xstage_keys = [[("hT", k) for k in range(16)], [("ya", k) for k in range(16)]]

    def layer(l, N, nseg, slots, gcfgs, first_tile, prefetch=None):
        L = N // nseg
        nsub = N // 128
        dma("sp", "lw", lwa[:].rearrange("p a b -> p (a b)"), la_s[l], writes=["lwa"])
        dma("sp", "lw", lwx[:].rearrange("p a b -> p (a b)"), lx_s[l], writes=["lwx"])
        gtiles = [bc[:], sgb[:].rearrange("p a b -> p (a b)")]
        gkeys = [["bc"], [("sgb", i) for i in range(4)]]
        for j in range(nsub):
            S.add("act", lambda e, j=j: e.activation(out=xn[j % 2][:], in_=xt[:, j, :], func=AF.Square,
                                                      accum_out=ss[:, j:j + 1]),
                  reads=[("xt", j)], writes=[("xn", 0), ("ss", j)])
        S.add("act", lambda e: e.activation(out=sd[:, 0:nsub], in_=ss[:, 0:nsub], func=AF.Sqrt,
                                            scale=1.0 / D, bias=eps_t[:, 0:1]),
              reads=[("ss", j) for j in range(nsub)] + ["eps"], writes=["sd"])
        S.add("dve", lambda e: e.reciprocal(out=rstd[:, 0:nsub], in_=sd[:, 0:nsub]), reads=["sd"], writes=["rstd"])
        for j in range(nsub):
            if DEBUG_STOP == 30:
                raise StopBuild()
            S.add("act", lambda e, j=j: e.activation(out=xn[j % 2][:], in_=xt[:, j, :], func=AF.Identity,
                                                      scale=rstd[:, j:j + 1]),
                  reads=[("xt", j), "rstd"], writes=[("xn", 0)])
            for k in range(16):
                bank = 6 + k // 8
                pv = ps[bank][:].bitcast(BF16)[:, (k % 8) * 128:(k % 8 + 1) * 128]
                S.add("pe", lambda e, pv=pv, j=j, k=k: e.transpose(out=pv, in_=xn[j % 2][:, k * 128:(k + 1) * 128],
                                                                    identity=identb[:]),
                      reads=[("xn", 0), "identb"], writes=[PS(bank)])
            if DEBUG_STOP == 31:
                raise StopBuild()
            for k in range(16):
                bank = 6 + k // 8
                if nseg == 1:
                    parts = [(0, 128, slots[0])]
                else:
                    parts = [((s * L) % 128, L, slots[s]) for s in range(nseg) if (s * L) // 128 == j]
                for (o0, ln, slot) in parts:
                    pv = ps[bank][:].bitcast(BF16)[:, (k % 8) * 128 + o0:(k % 8) * 128 + o0 + ln]
                    engn = "dve" if (k < 8) else "act"
                    if engn == "dve":
                        S.add("dve", lambda e, pv=pv, j=j, k=k, o0=o0, ln=ln, slot=slot: e.tensor_scalar(
                            out=hT[:, k, j * 128 + o0:j * 128 + o0 + ln], in0=pv,
                            scalar1=sc1[:, l, k, slot:slot + 1], scalar2=shf[:, l, k, slot:slot + 1],
                            op0=ALU.mult, op1=ALU.add), reads=[PS(bank)], writes=[("hT", k)])
                    else:
                        S.add("act", lambda e, pv=pv, j=j, k=k, o0=o0, ln=ln, slot=slot: e.activation(
                            out=hT[:, k, j * 128 + o0:j * 128 + o0 + ln], in_=pv, func=AF.Identity,
                            scale=sc1[:, l, k, slot:slot + 1], bias=shf[:, l, k, slot:slot + 1]),
                            reads=[PS(bank)], writes=[("hT", k)])
        if DEBUG_STOP == 20:
            raise StopBuild()
        hkeys = [("hT", k) for k in range(16)]

        def proj(bank, wslot, wkeys, act, akeys):
            for k in range(16):
                S.add("pe", lambda e, k=k: e.matmul(ps[bank][:, 0:N], lhsT=ring[:, wslot, k * 128:(k + 1) * 128],
                                                    rhs=act[:, k, 0:N], start=(k == 0), stop=(k == 15)),
                      reads=wkeys + [akeys[k]], writes=[PS(bank)])

        def seg3(ap2d, H):
            return ap2d[:, 0:nseg * (H + L)].rearrange("p (s l) -> p s l", s=nseg)

        s0 = slots[0]
        BXA, BGA, BXB, BGB, BR, BI = 0, 1, 2, 3, 4, 5

        def tail(c):
            q = c % 2
            cv = tmp["cv"][q][:, 0:N]
            tr = tmp["tr"][0][:, 0:N]
            ti = tmp["ti"][0][:, 0:N]
            a2 = tmp["a2"][0][:, 0:N]
            hh = tmp["h"][0][:, 0:N]
            tg = tmp["tg"][q][:, 0:N]
            S.add("act", lambda e: e.activation(out=tr, in_=ps[BR][:, 0:N], func=AF.Tanh, scale=0.5,
                                                bias=bah[:, l, c:c + 1]),
                  reads=[PS(BR), "bah"], writes=["tr"])
            S.add("act", lambda e: e.activation(out=ti, in_=ps[BI][:, 0:N], func=AF.Tanh, scale=0.5,
                                                bias=bxh[:, l, c:c + 1]),
                  reads=[PS(BI), "bxh"], writes=["ti"])
            S.add("dve", lambda e: e.tensor_scalar(out=tr, in0=tr, scalar1=chh[:, l, c:c + 1],
                                                   scalar2=chh[:, l, c:c + 1], op0=ALU.mult, op1=ALU.add),
                  reads=["tr", "chh"], writes=["tr"])
            S.add("act", lambda e: e.activation(out=a2, in_=tr, func=AF.Exp, scale=2.0),
                  reads=["tr"], writes=["a2"])
            S.add("act", lambda e: e.activation(out=tr, in_=tr, func=AF.Exp), reads=["tr"], writes=["tr"])
            S.add("dve", lambda e: e.tensor_scalar(out=a2, in0=a2, scalar1=1.0, scalar2=None, op0=ALU.min),
                  reads=["a2"], writes=["a2"])
            S.add("act", lambda e: e.activation(out=a2, in_=a2, func=AF.Sqrt, scale=-0.25, bias=q25[:, 0:1]),
                  reads=["a2", "q25"], writes=["a2"])
            S.add("dve", lambda e: e.scalar_tensor_tensor(out=ti, in0=ti, scalar=1.0, in1=cv, op0=ALU.add,
                                                          op1=ALU.mult),
                  reads=["ti", ("cv", q)], writes=["ti"])
            S.add("dve", lambda e: e.tensor_tensor(out=ti, in0=ti, in1=a2, op=ALU.mult),
                  reads=["ti", "a2"], writes=["ti"])
            for s in range(nseg):
                S.add("dve", lambda e, s=s: e.tensor_tensor_scan(
                    out=hh[:, s * L:(s + 1) * L], data0=tr[:, s * L:(s + 1) * L], data1=ti[:, s * L:(s + 1) * L],
                    initial=Slru[:, l, slots[s], c:c + 1], op0=ALU.mult, op1=ALU.add),
                    reads=["tr", "ti", ("Slru", l, c)], writes=["h"])
            h3 = hh.rearrange("p (s l) -> p s l", s=nseg)
            S.add("pool", lambda e: e.tensor_copy(out=Slru[:, l, s0:s0 + nseg, c:c + 1], in_=h3[:, :, L - 1:L]),
                  reads=["h"], writes=[("Slru", l, c)])
            S.add("dve", lambda e: e.scalar_tensor_tensor(out=ya[:, c, 0:N], in0=tg, scalar=0.5, in1=hh,
                                                          op0=ALU.mult, op1=ALU.mult),
                  reads=[("tg", q), "h"], writes=[("ya", c)])

        def gates(c):
            q = c % 2
            S.add("pe", lambda e: e.matmul(ps[BR][:, 0:N], lhsT=lwa[:, c, :], rhs=cvb[q][:, 0:N],
                                           start=True, stop=True),
                  reads=["lwa", ("cvb", q)], writes=[PS(BR)])
            S.add("pe", lambda e: e.matmul(ps[BI][:, 0:N], lhsT=lwx[:, c, :], rhs=cvb[q][:, 0:N],
                                           start=True, stop=True),
                  reads=["lwx", ("cvb", q)], writes=[PS(BI)])

        def pool_mm(g, pw_slot, pw_keys):
            pwv = ring[:, pw_slot, :].rearrange("p (a b) -> p a b", a=4)
            for ec in range(4):
                bo = 6 + ec % 2
                for c2 in range(4):
                    S.add("pe", lambda e, bo=bo, c2=c2, ec=ec: e.matmul(
                        ps[bo][:, 0:N], lhsT=pwv[:, c2, ec * 128:(ec + 1) * 128], rhs=dbf[:, c2, 0:N],
                        start=(c2 == 0), stop=(c2 == 3)), reads=pw_keys + [("dbf", c2)], writes=[PS(bo)])
                oc = g * 4 + ec
                S.add("dve", lambda e, bo=bo, oc=oc, ec=ec: e.scalar_tensor_tensor(
                    out=yb[:, oc, 0:N], in0=ps[bo][:, 0:N], scalar=psh[:, l, oc:oc + 1],
                    in1=sgb[:, ec, 0:N], op0=ALU.mult, op1=ALU.mult),
                    reads=[PS(bo), "psh", ("sgb", ec)], writes=[("yb", oc)])

        for c in range(16):
            q = c % 2
            g = c // 4
            cc = c % 4
            if cc == 3:
                pend_pw = (g,) + ring_load(pw_s[l][:, g * 2048:(g + 1) * 2048])
            wx_slot, wx_keys = ring_load(win_s[l, c])
            wg_slot, wg_keys = ring_load(win_s[l, 16 + c])
            proj(BXA, wx_slot, wx_keys, hT, hkeys)
            proj(BGA, wg_slot, wg_keys, hT, hkeys)
            if c > 0:
                gates(c - 1)
            if cc == 0 and c > 0:
                pool_mm(*pend_pw)
            wb_slot, wb_keys = ring_load(win_s[l, 32 + c])
            wh_slot, wh_keys = ring_load(win_s[l, 48 + c])
            proj(BXB, wb_slot, wb_keys, hT, hkeys)
            proj(BGB, wh_slot, wh_keys, hT, hkeys)
            e3 = seg3(ext[0], 3)
            S.add("pool", lambda e, e3=e3, c=c: e.tensor_copy(out=e3[:, :, 0:3],
                                                             in_=Sconv[:, l, s0:s0 + nseg, c, :]),
                  reads=[("Sconv", l, c)], writes=["exth"])
            S.add("act", lambda e, e3=e3: e.activation(
                out=e3[:, :, 3:3 + L], in_=ps[BXA][:, 0:N].rearrange("p (s l) -> p s l", s=nseg), func=AF.Copy),
                reads=[PS(BXA)], writes=["ext"])
            cv = tmp["cv"][q][:, 0:N]
            cv3 = cv.rearrange("p (s l) -> p s l", s=nseg)
            S.add("act", lambda e, cv=cv, c=c: e.activation(
                out=cv, in_=ps[BXA][:, 0:N], func=AF.Identity, scale=vecs[:, l, 4, c:c + 1],
                bias=vecs[:, l, 5, c:c + 1]), reads=[PS(BXA), "vecs"], writes=[("cv", q)])
            S.add("pool", lambda e, e3=e3, c=c: e.tensor_copy(out=Sconv[:, l, s0:s0 + nseg, c, :],
                                                             in_=e3[:, :, L:L + 3]),
                  reads=["ext", "exth"], writes=[("Sconv", l, c)])
            for kk in range(3):
                S.add("dve", lambda e, cv3=cv3, e3=e3, kk=kk, c=c: e.scalar_tensor_tensor(
                    out=cv3, in0=e3[:, :, kk:kk + L], scalar=vecs[:, l, 1 + kk, c:c + 1], in1=cv3,
                    op0=ALU.mult, op1=ALU.add), reads=["ext", "exth", ("cv", q)], writes=[("cv", q)])
            tg = tmp["tg"][q][:, 0:N]
            S.add("act", lambda e, tg=tg: e.activation(out=tg, in_=ps[BGA][:, 0:N], func=AF.Tanh, scale=0.5),
                  reads=[PS(BGA)], writes=[("tg", q)])
            S.add("act", lambda e, cv=cv, q=q: e.activation(out=cvb[q][:, 0:N], in_=cv, func=AF.Copy),
                  reads=[("cv", q)], writes=[("cvb", q)])
            S.add("dve", lambda e, tg=tg: e.scalar_tensor_tensor(
                out=tg, in0=tg, scalar=1.0, in1=ps[BGA][:, 0:N], op0=ALU.add, op1=ALU.mult),
                reads=[("tg", q), PS(BGA)], writes=[("tg", q)])
            p3 = seg3(pext[0], 15)
            S.add("pool", lambda e, p3=p3, c=c: e.tensor_copy(out=p3[:, :, 0:15],
                                                             in_=Spool[:, l, s0:s0 + nseg, c, :]),
                  reads=[("Spool", l, c)], writes=["pexth"])
            S.add("act", lambda e, p3=p3: e.activation(
                out=p3[:, :, 15:15 + L], in_=ps[BXB][:, 0:N].rearrange("p (s l) -> p s l", s=nseg), func=AF.Copy),
                reads=[PS(BXB)], writes=["pext"])
            S.add("pool", lambda e, p3=p3, c=c: e.tensor_copy(out=Spool[:, l, s0:s0 + nseg, c, :],
                                                             in_=p3[:, :, L:L + 15]),
                  reads=["pext", "pexth"], writes=[("Spool", l, c)])
            A3 = seg3(pA, 15)
            B3 = seg3(pB, 15)
            src = p3
            srck = ["pext", "pexth"]
            bufs = [(A3, "pA"), (B3, "pB")]
            sh = 1
            lo = 0
            for step in range(g + 1):
                dst, dk = bufs[step % 2]
                lo2 = lo + sh
                S.add("pool", lambda e, dst=dst, src=src, lo2=lo2, sh=sh: e.tensor_tensor(
                    out=dst[:, :, lo2:15 + L], in0=src[:, :, lo2:15 + L], in1=src[:, :, lo2 - sh:15 + L - sh],
                    op=ALU.add), reads=srck, writes=[dk])
                src, srck = dst, [dk]
                lo = lo2
                sh *= 2
            wwin = float(2 ** (g + 1))
            d3 = dbf[:, cc, 0:N].rearrange("p (s l) -> p s l", s=nseg)
            S.add("dve", lambda e, src=src, p3=p3, d3=d3, wwin=wwin: e.scalar_tensor_tensor(
                out=d3, in0=src[:, :, 15:15 + L], scalar=1.0 / wwin, in1=p3[:, :, 15:15 + L],
                op0=ALU.mult, op1=ALU.subtract), reads=srck + ["pext"], writes=[("dbf", cc)])
            if first_tile:
                S.add("dve", lambda e, src=src, g=g: e.tensor_tensor(
                    out=small[:, 0:16], in0=src[:, 0, 15:31], in1=invc[:, g, :], op=ALU.mult),
                    reads=srck + ["invc"], writes=["small16"])
                S.add("dve", lambda e, p3=p3, cc=cc: e.tensor_tensor(
                    out=dbf[:, cc, 0:16], in0=small[:, 0:16], in1=p3[:, 0, 15:31], op=ALU.subtract),
                    reads=["small16", "pext"], writes=[("dbf", cc)])
            tgb = tmp["tgb"][0][:, 0:N]
            S.add("act", lambda e, tgb=tgb: e.activation(out=tgb, in_=ps[BGB][:, 0:N], func=AF.Tanh, scale=0.5),
                  reads=[PS(BGB)], writes=["tgb"])
            S.add("dve", lambda e, tgb=tgb, cc=cc: e.scalar_tensor_tensor(
                out=sgb[:, cc, 0:N], in0=tgb, scalar=1.0, in1=ps[BGB][:, 0:N], op0=ALU.add, op1=ALU.mult),
                reads=["tgb", PS(BGB)], writes=[("sgb", cc)])
            if c > 0:
                tail(c - 1)
        pool_mm(*pend_pw)
        gates(15)
        tail(15)
        if DEBUG_STOP == 22:
            raise StopBuild()
        yakeys = [("ya", k) for k in range(16)]
        ybkeys = [("yb", k) for k in range(16)]
        for c in range(16):
            base = 0 if c % 2 == 0 else 4
            s3, k3 = ring_load(win_s[l, 64 + c])
            s4, k4 = ring_load(win_s[l, 80 + c])
            s1, k1 = ring_load(pa_s[l, c])
            s2, k2 = ring_load(pb_s[l, c])
            proj(base + 2, s3, k3, hT, hkeys)
            proj(base + 3, s4, k4, hT, hkeys)
            proj(base + 0, s1, k1, ya, yakeys)
            proj(base + 1, s2, k2, yb, ybkeys)
            q = 0
            t1 = tmp["tr"][q][:, 0:N]
            t2 = tmp["ti"][q][:, 0:N]
            S.add("act", lambda e, t1=t1, base=base: e.activation(out=t1, in_=ps[base + 2][:, 0:N], func=AF.Tanh,
                                                                  scale=0.5),
                  reads=[PS(base + 2)], writes=["tr"])
            S.add("act", lambda e, t2=t2, base=base: e.activation(out=t2, in_=ps[base + 3][:, 0:N], func=AF.Tanh,
                                                                  scale=0.5),
                  reads=[PS(base + 3)], writes=["ti"])
            S.add("dve", lambda e, t1=t1, base=base: e.scalar_tensor_tensor(
                out=t1, in0=t1, scalar=1.0, in1=ps[base + 0][:, 0:N], op0=ALU.add, op1=ALU.mult),
                reads=["tr", PS(base + 0)], writes=["tr"])
            S.add("dve", lambda e, t2=t2, base=base: e.scalar_tensor_tensor(
                out=t2, in0=t2, scalar=1.0, in1=ps[base + 1][:, 0:N], op0=ALU.add, op1=ALU.mult),
                reads=["ti", PS(base + 1)], writes=["ti"])
            S.add("dve", lambda e, t1=t1, t2=t2, c=c: e.tensor_tensor(out=mT[:, c, 0:N], in0=t1, in1=t2,
                                                                      op=ALU.add),
                  reads=["tr", "ti"], writes=[("mT", c)])
        if DEBUG_STOP == 23:
            raise StopBuild()
        mkeys = [("mT", k) for k in range(16)]
        bi_ = 0
        for gi, cfg in enumerate(gcfgs):
            S.add("pool", lambda e, gi=gi, cfg=cfg: e.dma_start(out=gtiles[gi], in_=gbc_s[l, cfg]),
                  writes=gkeys[gi], dma=("gbc", gi))
        for nb in range(4):
            wslot, wkeys = ring_load(wo_s[l, nb], nslots=4)
            if prefetch is not None and nb == 1:
                src_rows, nsub_n = prefetch
                for j in range(nsub_n):
                    dma("sp", "xin", xstage[j // 2][:, (j % 2) * D:(j % 2 + 1) * D],
                        src_rows[j * 128:(j + 1) * 128, :], writes=xstage_keys[j // 2])
            wv = ring[:, wslot:wslot + 4, :].rearrange("p a b -> p (a b)").rearrange("p (k n) -> p k n", k=16)
            for j in range(nsub):
                bank = bi_ % 8
                bi_ += 1
                for k in range(16):
                    S.add("pe", lambda e, bank=bank, j=j, k=k, wv=wv: e.matmul(
                        ps[bank][:, :], lhsT=mT[:, k, j * 128:(j + 1) * 128], rhs=wv[:, k, :],
                        start=(k == 0), stop=(k == 15)), reads=wkeys + [mkeys[k]], writes=[PS(bank)])
                gi = 0 if len(gcfgs) == 1 else j
                rs = tmp["rs"][0]
                S.add("dve", lambda e, bank=bank, rs=rs, gi=gi, nb=nb: e.tensor_tensor(
                    out=rs[:], in0=ps[bank][:, :], in1=gtiles[gi][:, nb * 512:(nb + 1) * 512], op=ALU.mult),
                    reads=[PS(bank)] + gkeys[gi], writes=[("rs", 0)])
                S.add("dve", lambda e, rs=rs, j=j, nb=nb: e.tensor_tensor(
                    out=xt[:, j, nb * 512:(nb + 1) * 512], in0=xt[:, j, nb * 512:(nb + 1) * 512], in1=rs[:],
                    op=ALU.add), reads=[("rs", 0), ("xt", j)], writes=[("xt", j)])

    eps_t = sb("eps_t", [128, 1])
    S.add("dve", lambda e: e.memset(eps_t[:], EPS), writes=["eps"])
    S.add("dve", lambda e: e.memset(q25[:], 0.25), writes=["q25"])

    def final_norm(nsub, dst_rows):
        for j in range(nsub):
            S.add("act", lambda e, j=j: e.activation(out=xn[j % 2][:], in_=xt[:, j, :], func=AF.Square,
                                                      accum_out=ss[:, j:j + 1]),
                  reads=[("xt", j)], writes=[("xn", 0), ("ss", j)])
        S.add("act", lambda e: e.activation(out=sd[:, 0:nsub], in_=ss[:, 0:nsub], func=AF.Sqrt,
                                            scale=1.0 / D, bias=eps_t[:, 0:1]),
              reads=[("ss", j) for j in range(nsub)] + ["eps"], writes=["sd"])
        S.add("dve", lambda e: e.reciprocal(out=rstd[:, 0:nsub], in_=sd[:, 0:nsub]), reads=["sd"], writes=["rstd"])
        S.add("pool", lambda e: e.dma_start(out=bc[:], in_=fgb_d), writes=["bc"], dma=("gbc", 0))
        ybuf = [yb[:].rearrange("p a b -> p (a b)").bitcast(F32), mT[:].rearrange("p a b -> p (a b)").bitcast(F32)]
        yk = [[("yb", k) for k in range(16)], [("mT", k) for k in range(16)]]
        for j in range(nsub):
            o = ybuf[j // 2][:, (j % 2) * D:(j % 2 + 1) * D]
            S.add("dve", lambda e, j=j, o=o: e.scalar_tensor_tensor(
                out=o, in0=xt[:, j, :], scalar=rstd[:, j:j + 1], in1=bc[:], op0=ALU.mult, op1=ALU.mult),
                reads=[("xt", j), "rstd", "bc"], writes=yk[j // 2])
            S.add("pool", lambda e, j=j, o=o: e.dma_start(out=dst_rows[j * 128:(j + 1) * 128, :], in_=o),
                  reads=yk[j // 2], dma="yout")

    try:
        if DEBUG_STOP >= 20:
            dma("sp", "xin", xt[:, 0:4, :], xp[0:512, :].rearrange("(j p) d -> p j d", p=128),
                writes=[("xt", j) for j in range(4)])
            layer(0, 512, 1, [0], [0], True)
    except StopBuild:
        S.final_wait()
        S.emit(nc, es)
        es.close()
        return nc
    def x_from_stage(nsub_n):
        for j in range(nsub_n):
            S.add("pool", lambda e, j=j: e.tensor_copy(out=xt[:, j, :],
                                                       in_=xstage[j // 2][:, (j % 2) * D:(j % 2 + 1) * D]),
                  reads=xstage_keys[j // 2], writes=[("xt", j)])

    for t in range(NT_P):
        if t == 0:
            dma("sp", "xin", xt[:, 0:4, :], xp[0:512, :].rearrange("(j p) d -> p j d", p=128),
                writes=[("xt", j) for j in range(4)])
        else:
            x_from_stage(4)
        nxt = (xp[(t + 1) * 512:(t + 2) * 512, :], 4) if t + 1 < NT_P else (xs, 2)
        for l in range(2):
            layer(l, 512, 1, [0], [0], t == 0, prefetch=(nxt if l == 1 else None))
            if DEBUG_STOP == 1:
                break
        if DEBUG_STOP == 1:
            S.final_wait()
            S.emit(nc, es)
            es.close()
            return nc
        final_norm(4, yp[t * 512:(t + 1) * 512, :])
    x_from_stage(2)
    for l in range(2):
        layer(l, 256, 4, [1, 2, 3, 4], [1, 2], False)
    final_norm(2, ys)
    S.add("pool", lambda e: e.dma_start(out=oconv_d, in_=Sconv[:].rearrange("p a b c d -> p (a b c d)")),
          reads=[("Sconv", l_, c_) for l_ in range(2) for c_ in range(16)], dma="sout")
    S.add("pool", lambda e: e.dma_start(out=opool_d, in_=Spool[:].rearrange("p a b c d -> p (a b c d)")),
          reads=[("Spool", l_, c_) for l_ in range(2) for c_ in range(16)], dma="sout")
    S.add("pool", lambda e: e.dma_start(out=olru_d, in_=Slru[:].rearrange("p a b c -> p (a b c)")),
          reads=[("Slru", l_, c_) for l_ in range(2) for c_ in range(16)], dma="sout")
    S.final_wait()
    S.emit(nc, es)
    es.close()
    return nc


_NC = None


def _get_nc():
    global _NC
    if _NC is None:
        _NC = build_program()
    return _NC


def _fm(v):
    v = np.asarray(v, np.float32)
    lead = v.shape[:-1]
    r = v.reshape(*lead, NCH, 128)
    return np.ascontiguousarray(np.moveaxis(r, -1, 0))


def kernel(x_prompt, x_sample, c_prompt, c_sample, state_conv, state_lru, state_pool,
           norm_g, w_ada, b_ada, w_in, conv_w, conv_b, lru_wa, lru_ba, lru_wx, lru_bx,
           lru_lam, pool_w, pool_scale, w_proj_a, w_proj_b, w_out, final_g):
    f32 = np.float32
    x_prompt = np.asarray(x_prompt, f32)
    x_sample = np.asarray(x_sample, f32)
    nc = _get_nc()
    vecs = np.zeros((128, 2, 10, NCH), f32)
    for l in range(2):
        rows = [norm_g[l], conv_w[l][0], conv_w[l][1], conv_w[l][2], conv_w[l][3], conv_b[l],
                lru_ba[l], lru_bx[l], lru_lam[l], pool_scale[l]]
        for v, r in enumerate(rows):
            vecs[:, l, v, :] = np.asarray(r, f32).reshape(NCH, 128).T
    b_ada = np.asarray(b_ada, f32)
    badaT = np.ascontiguousarray(b_ada[:, :4096].reshape(2, 32, 128).transpose(2, 0, 1))
    bgate = np.ascontiguousarray(np.broadcast_to(b_ada[None, :, 4096:], (5, 2, D)))
    fgb = np.ascontiguousarray(np.broadcast_to(np.asarray(final_g, f32)[None, :], (128, D)))
    sel = np.zeros((5, 3, 128), f32)
    sel[0, 0, :] = 1.0
    sel[1, 1, :64] = 1.0
    sel[2, 1, 64:] = 1.0
    sel[3, 2, :64] = 1.0
    sel[4, 2, 64:] = 1.0
    invc = np.zeros((128, 4, 16), f32)
    for g, w in enumerate((2, 4, 8, 16)):
        invc[:, g, :] = 1.0 / np.minimum(w, np.arange(16) + 1).astype(f32)
    ident = np.eye(128, dtype=f32)
    shared = {
        "vecs": vecs.reshape(128, -1), "badaT": badaT.reshape(128, -1), "bgate": bgate.reshape(5, -1),
        "fgb": fgb, "sel": sel.reshape(5, -1), "invc": invc.reshape(128, -1), "ident": ident,
        "w_ada": np.asarray(w_ada, f32), "w_in": np.asarray(w_in, f32),
        "lru_wa": np.asarray(lru_wa, f32).reshape(2, D, 128), "lru_wx": np.asarray(lru_wx, f32).reshape(2, D, 128),
        "pool_w": np.asarray(pool_w, f32).reshape(2, D, 512),
        "w_proj_a": np.asarray(w_proj_a, f32), "w_proj_b": np.asarray(w_proj_b, f32), "w_out": np.asarray(w_out, f32),
    }
    zeros_seq = np.zeros((SEQ, D), f32)
    state_conv = np.asarray(state_conv, f32)
    state_pool = np.asarray(state_pool, f32)
    state_lru = np.asarray(state_lru, f32)
    in_maps = []
    for k in range(NCORES):
        m = dict(shared)
        m["xp"] = x_prompt[k] if k < 2 else zeros_seq
        m["xs"] = np.ascontiguousarray(x_sample[4 * k:4 * k + 4].reshape(256, D))
        c5 = np.concatenate([np.asarray(c_prompt, f32)[k % 2][None], np.asarray(c_sample, f32)[4 * k:4 * k + 4]], 0)
        m["c5T"] = np.ascontiguousarray(c5.reshape(5, NCH, 128).transpose(2, 1, 0)).reshape(128, -1)
        sc = state_conv[:, 4 * k:4 * k + 4]
        m["sconvT"] = np.ascontiguousarray(sc.reshape(2, 4, 3, NCH, 128).transpose(4, 0, 1, 3, 2)).reshape(128, -1)
        sp_ = state_pool[:, 4 * k:4 * k + 4]
        m["spoolT"] = np.ascontiguousarray(sp_.reshape(2, 4, 15, NCH, 128).transpose(4, 0, 1, 3, 2)).reshape(128, -1)
        sl = state_lru[:, 4 * k:4 * k + 4]
        m["slruT"] = np.ascontiguousarray(sl.reshape(2, 4, NCH, 128).transpose(3, 0, 1, 2)).reshape(128, -1)
        in_maps.append(m)
    res = run_bass_kernel_spmd(nc, in_maps, core_ids=list(range(NCORES)))
    R = res.results
    y_prompt = np.stack([np.asarray(R[k]["yp"], f32) for k in range(2)], 0)
    y_sample = np.concatenate([np.asarray(R[k]["ys"], f32).reshape(4, 64, D) for k in range(NCORES)], 0)

    def unfm(a, tail):
        a = np.asarray(a, f32).reshape(128, 2, 5, NCH, *tail)
        if tail:
            a = a.transpose(1, 2, 4, 3, 0)
        else:
            a = a.transpose(1, 2, 3, 0)
        return np.ascontiguousarray(a).reshape(2, 5, *tail, D)

    oc = [unfm(R[k]["o_conv"], (3,)) for k in range(NCORES)]
    op = [unfm(R[k]["o_pool"], (15,)) for k in range(NCORES)]
    ol = [unfm(R[k]["o_lru"], ()) for k in range(NCORES)]
    conv_p = np.stack([oc[k][:, 0] for k in range(2)], 1)
    lru_p = np.stack([ol[k][:, 0] for k in range(2)], 1)
    pool_p = np.stack([op[k][:, 0] for k in range(2)], 1)
    conv_s = np.concatenate([oc[k][:, 1:5] for k in range(NCORES)], 1)
    lru_s = np.concatenate([ol[k][:, 1:5] for k in range(NCORES)], 1)
    pool_s = np.concatenate([op[k][:, 1:5] for k in range(NCORES)], 1)
    return (y_prompt, y_sample, conv_p, lru_p, pool_p, conv_s, lru_s, pool_s)
```

```python
from contextlib import ExitStack
import numpy as np
import concourse.bass as bass
import concourse.mybir as mybir
from concourse.bass_utils import run_bass_kernel_spmd

F32 = mybir.dt.float32
BF16 = mybir.dt.bfloat16
AF = mybir.ActivationFunctionType
ALU = mybir.AluOpType

D = 2048
NCH = 16
SEQ = 16384
NT_P = SEQ // 512
DEPTH = 2
NCORES = 8
EPS = 1e-6
SEM_LIMIT = 30000
DEBUG_STOP = 0


class StopBuild(Exception):
    pass


class Op:
    __slots__ = ("eng", "fn", "waits", "signal", "sem", "cnt", "dma", "inc")

    def __init__(self, eng, fn, dma):
        self.eng = eng
        self.fn = fn
        self.waits = []
        self.signal = False
        self.sem = None
        self.cnt = 0
        self.dma = dma
        self.inc = 0


class Sched:
    ENGS = ("pe", "act", "dve", "pool", "sp")

    def __init__(self):
        self.ops = {e: [] for e in self.ENGS}
        self.lastw = {}
        self.readers = {}
        self.dmacount = {}

    def add(self, eng, fn, reads=(), writes=(), dma=None):
        op = Op(eng, fn, dma)
        deps = []
        for k in reads:
            w = self.lastw.get(k)
            if w is not None:
                deps.append((w, True))
            if isinstance(k, tuple) and k[0] == "ps":
                for r in self.readers.get(k, ()):
                    if r.eng != eng:
                        deps.append((r, False))
        for k in writes:
            w = self.lastw.get(k)
            if w is not None:
                deps.append((w, False))
            for r in self.readers.get(k, ()):
                deps.append((r, False))
        for (y, raw) in deps:
            if y is op:
                continue
            if y.dma is not None:
                op.waits.append(("dma", y.dma, 16 * self.dmacount[y.dma]))
            else:
                if y.eng == eng and eng == "pe":
                    continue
                y.signal = True
                op.waits.append(("eng", y))
        for k in writes:
            self.lastw[k] = op
            self.readers[k] = []
        for k in reads:
            self.readers.setdefault(k, []).append(op)
        if dma is not None:
            self.dmacount[dma] = self.dmacount.get(dma, 0) + 1
        self.ops[eng].append(op)
        return op

    def fence(self):
        lasts = []
        for e in self.ENGS:
            real = [o for o in self.ops[e] if o.fn is not None and o.dma is None]
            if real:
                lasts.append(real[-1])
        dm = dict(self.dmacount)
        for e in self.ENGS:
            op = Op(e, None, None)
            for y in lasts:
                if y.dma is None and y.eng != e:
                    y.signal = True
                    op.waits.append(("eng", y))
            for k, c in dm.items():
                op.waits.append(("dma", k, 16 * c))
            self.ops[e].append(op)
        self.lastw = {}
        self.readers = {}

    def final_wait(self):
        op = Op("sp", None, None)
        for k, c in self.dmacount.items():
            op.waits.append(("dma", k, 16 * c))
        for e in self.ENGS:
            if e == "sp":
                continue
            real = [o for o in self.ops[e] if o.fn is not None and o.dma is None]
            if real:
                real[-1].signal = True
                op.waits.append(("eng", real[-1]))
        self.ops["sp"].append(op)

    def emit(self, nc, es):
        engsems = {}
        for e in self.ENGS:
            cur = None
            cnt = 0
            n = 0
            for op in self.ops[e]:
                if op.dma is not None or not op.signal:
                    continue
                if cur is None or cnt >= SEM_LIMIT:
                    cur = es.enter_context(nc.semaphore(f"s_{e}_{n}"))
                    n += 1
                    cnt = 0
                cnt += 1
                op.sem = cur
                op.cnt = cnt
        dmasems = {k: es.enter_context(nc.semaphore(f"d_{i}")) for i, k in enumerate(self.dmacount)}
        block = es.enter_context(nc.Block())

        def run(ename, eng):
            waited = {}
            for op in self.ops[ename]:
                for w in op.waits:
                    if w[0] == "dma":
                        sem, val = dmasems[w[1]], w[2]
                    else:
                        sem, val = w[1].sem, w[1].cnt
                    key = id(sem)
                    if waited.get(key, 0) >= val:
                        continue
                    waited[key] = val
                    eng.wait_ge(sem, val)
                if op.fn is None:
                    continue
                inst = op.fn(eng)
                if op.dma is not None:
                    inst.then_inc(dmasems[op.dma], 16)
                elif op.signal:
                    inst.then_inc(op.sem, 1)

        @block.tensor
        def _(eng):
            run("pe", eng)

        @block.scalar
        def _(eng):
            run("act", eng)

        @block.vector
        def _(eng):
            run("dve", eng)

        @block.gpsimd
        def _(eng):
            run("pool", eng)

        @block.sync
        def _(eng):
            run("sp", eng)


def build_program():
    nc = bass.Bass("TRN2", target_bir_lowering=False)
    S = Sched()
    es = ExitStack()

    def din(name, shape, dt=F32):
        return nc.dram_tensor(name, list(shape), dt, kind="ExternalInput").ap()

    def dout(name, shape, dt=F32):
        return nc.dram_tensor(name, list(shape), dt, kind="ExternalOutput").ap()

    def dint(name, shape, dt):
        return nc.dram_tensor(name, list(shape), dt, kind="Internal").ap()

    xp = din("xp", [SEQ, D])
    xs = din("xs", [256, D])
    c5T_d = din("c5T", [128, NCH * 5])
    vecs_d = din("vecs", [128, 2 * 10 * NCH])
    badaT_d = din("badaT", [128, 2 * 32])
    bgate_d = din("bgate", [5, 2 * D])
    sconv_d = din("sconvT", [128, 2 * 4 * NCH * 3])
    spool_d = din("spoolT", [128, 2 * 4 * NCH * 15])
    slru_d = din("slruT", [128, 2 * 4 * NCH])
    fgb_d = din("fgb", [128, D])
    sel_d = din("sel", [5, 3 * 128])
    invc_d = din("invc", [128, 4 * 16])
    ident_d = din("ident", [128, 128])
    w_ada = din("w_ada", [2, D, 3 * D])
    w_in = din("w_in", [2, D, 6 * D])
    lru_wa = din("lru_wa", [2, D, 128])
    lru_wx = din("lru_wx", [2, D, 128])
    pool_w = din("pool_w", [2, D, 512])
    w_pa = din("w_proj_a", [2, D, D])
    w_pb = din("w_proj_b", [2, D, D])
    w_out = din("w_out", [2, D, D])

    yp = dout("yp", [SEQ, D])
    ys = dout("ys", [256, D])
    oconv_d = dout("o_conv", [128, 2 * 5 * NCH * 3])
    opool_d = dout("o_pool", [128, 2 * 5 * NCH * 15])
    olru_d = dout("o_lru", [128, 2 * 5 * NCH])

    win_s = dint("win_s", [2, 96, 128, 2048], BF16)
    pa_s = dint("pa_s", [2, 16, 128, 2048], BF16)
    pb_s = dint("pb_s", [2, 16, 128, 2048], BF16)
    wo_s = dint("wo_s", [2, 4, 128, 8192], BF16)
    pw_s = dint("pw_s", [2, 128, 8192], BF16)
    la_s = dint("la_s", [2, 128, 2048], BF16)
    lx_s = dint("lx_s", [2, 128, 2048], BF16)
    gbc_s = dint("gbc_s", [2, 3, 128, D], F32)

    def sb(name, shape, dt=F32):
        return es.enter_context(nc.sbuf_tensor(name, list(shape), dt))

    NS = 8
    xt = sb("xt", [128, 4, D])
    ring = sb("ring", [128, NS, 2048], BF16)
    hT = sb("hT", [128, NCH, 512], BF16)
    ya = sb("ya", [128, NCH, 512], BF16)
    yb = sb("yb", [128, NCH, 512], BF16)
    mT = sb("mT", [128, NCH, 512], BF16)
    lwa = sb("lwa", [128, NCH, 128], BF16)
    lwx = sb("lwx", [128, NCH, 128], BF16)
    xn1 = sb("xn", [128, D], BF16)
    xn = [xn1, xn1]
    bc = sb("bc", [128, D])
    dbf = sb("dbf", [128, 4, 512], BF16)
    sgb = sb("sgb", [128, 4, 512])
    tmpn = ["tr", "ti", "a2", "rs", "h", "tgb"]
    tmp = {n: [sb(f"{n}", [128, 512])] * 2 for n in tmpn}
    tmp["cv"] = [sb(f"cv{i}", [128, 512]) for i in range(2)]
    tmp["tg"] = [sb(f"tg{i}", [128, 512]) for i in range(2)]
    q25 = sb("q25", [128, 1])
    cvb = [sb(f"cvb{i}", [128, 512], BF16) for i in range(2)]
    ext1 = sb("ext", [128, 528])
    ext = [ext1, ext1]
    pext1 = sb("pext", [128, 528])
    pext = [pext1, pext1]
    pA = sb("pA", [128, 528])
    pB = sb("pB", [128, 528])
    identb = sb("identb", [128, 128], BF16)
    vecs = sb("vecs_t", [128, 2, 10, NCH])
    bah = sb("bah", [128, 2, NCH])
    bxh = sb("bxh", [128, 2, NCH])
    chh = sb("chh", [128, 2, NCH])
    psh = sb("psh", [128, 2, NCH])
    sc1 = sb("sc1", [128, 2, NCH, 5])
    shf = sb("shf", [128, 2, NCH, 5])
    Sconv = sb("Sconv", [128, 2, 5, NCH, 3])
    Spool = sb("Spool", [128, 2, 5, NCH, 15])
    Slru = sb("Slru", [128, 2, 5, NCH])
    invc = sb("invc_t", [128, 4, 16])
    ss = sb("ss", [128, 4])
    sd = sb("sd", [128, 4])
    rstd = sb("rstd", [128, 4])
    small = sb("small", [128, 64])
    c5T = tmp["tg"][0][:, 0:80].rearrange("p (a b) -> p a b", a=NCH)
    scT = tmp["tg"][0][:, 80:160].rearrange("p (a b) -> p a b", a=NCH)
    badaT = tmp["tg"][0][:, 160:224].rearrange("p (a b) -> p a b", a=2)
    modT = tmp["tg"][0][:, 224:384].rearrange("p (a b) -> p a b", a=32)
    bgate = yb[:].rearrange("p a b -> p (a b)").bitcast(F32)[0:5, :].rearrange("p (a b) -> p a b", a=2)
    grow = mT[:].rearrange("p a b -> p (a b)").bitcast(F32)[0:5, 0:D]
    sel = tmp["tg"][1][0:5, 0:384].rearrange("p (a b) -> p a b", a=3)
    identf = tmp["cv"][0][:, 0:128]

    ps = [es.enter_context(nc.psum_tensor(f"ps{i}", [128, 512], F32)) for i in range(8)]

    def PS(i):
        return ("ps", i)

    def dma(eng, key, out, in_, reads=(), writes=(), **kw):
        S.add(eng, lambda e: e.dma_start(out=out, in_=in_, **kw), reads=reads, writes=writes, dma=key)

    dma("sp", "c0", vecs[:].rearrange("p a b c -> p (a b c)"), vecs_d, writes=["vecs"])
    dma("sp", "c0", c5T[:].rearrange("p a b -> p (a b)"), c5T_d, writes=["c5T"])
    dma("sp", "c0", badaT[:].rearrange("p a b -> p (a b)"), badaT_d, writes=["badaT"])
    dma("sp", "c0", yb[:].rearrange("p a b -> p (a b)").bitcast(F32)[0:5, :], bgate_d, writes=["bgate"])
    dma("sp", "c0", sel[:].rearrange("p a b -> p (a b)"), sel_d, writes=["sel"])
    dma("sp", "c0", invc[:].rearrange("p a b -> p (a b)"), invc_d, writes=["invc"])
    dma("sp", "c0", identf[:], ident_d, writes=["identf"])
    S.add("dve", lambda e: e.memset(Sconv[:].rearrange("p a b c d -> p (a b c d)"), 0.0), writes=["Sconv0"])
    S.add("dve", lambda e: e.memset(Spool[:].rearrange("p a b c d -> p (a b c d)"), 0.0), writes=["Spool0"])
    S.add("dve", lambda e: e.memset(Slru[:].rearrange("p a b c -> p (a b c)"), 0.0), writes=["Slru0"])
    for l in range(2):
        dma("sp", "c0", Sconv[:, l, 1:5].rearrange("p s c r -> p (s c r)"),
            sconv_d[:, l * 192:(l + 1) * 192], reads=["Sconv0"], writes=[("Sconv", l)])
        dma("sp", "c0", Spool[:, l, 1:5].rearrange("p s c r -> p (s c r)"),
            spool_d[:, l * 960:(l + 1) * 960], reads=["Spool0"], writes=[("Spool", l)])
        dma("sp", "c0", Slru[:, l, 1:5].rearrange("p s c -> p (s c)"),
            slru_d[:, l * 64:(l + 1) * 64], reads=["Slru0"], writes=[("Slru", l)])
    S.add("dve", lambda e: e.tensor_copy(out=identb[:], in_=identf[:]), reads=["identf"], writes=["identb"])
    for l in range(2):
        S.add("dve", lambda e, l=l: e.tensor_scalar(out=bah[:, l, :], in0=vecs[:, l, 6, :], scalar1=0.5,
                                                     scalar2=None, op0=ALU.mult), reads=["vecs"], writes=["bah"])
        S.add("dve", lambda e, l=l: e.tensor_scalar(out=bxh[:, l, :], in0=vecs[:, l, 7, :], scalar1=0.5,
                                                     scalar2=None, op0=ALU.mult), reads=["vecs"], writes=["bxh"])
        S.add("dve", lambda e, l=l: e.tensor_scalar(out=psh[:, l, :], in0=vecs[:, l, 9, :], scalar1=0.5,
                                                     scalar2=None, op0=ALU.mult), reads=["vecs"], writes=["psh"])
        S.add("act", lambda e, l=l: e.activation(out=small[:, l * 16:(l + 1) * 16], in_=vecs[:, l, 8, :],
                                                  func=AF.Exp, scale=-1.0), reads=["vecs"], writes=[("small", l)])
        S.add("act", lambda e, l=l: e.activation(out=small[:, 32 + l * 16:32 + (l + 1) * 16],
                                                  in_=small[:, l * 16:(l + 1) * 16], func=AF.Ln, bias=1.0),
              reads=[("small", l)], writes=[("small2", l)])
        S.add("dve", lambda e, l=l: e.tensor_scalar(out=chh[:, l, :], in0=small[:, 32 + l * 16:32 + (l + 1) * 16],
                                                     scalar1=-4.0, scalar2=None, op0=ALU.mult),
              reads=[("small2", l)], writes=["chh"])
    S.add("act", lambda e: e.activation(out=scT[:].rearrange("p a b -> p (a b)"),
                                        in_=c5T[:].rearrange("p a b -> p (a b)"), func=AF.Tanh, scale=0.5),
          reads=["c5T"], writes=["scT0"])
    S.add("dve", lambda e: e.scalar_tensor_tensor(out=scT[:].rearrange("p a b -> p (a b)"),
                                                  in0=scT[:].rearrange("p a b -> p (a b)"), scalar=1.0,
                                                  in1=c5T[:].rearrange("p a b -> p (a b)"),
                                                  op0=ALU.add, op1=ALU.mult),
          reads=["scT0", "c5T"], writes=["scT1"])
    S.add("dve", lambda e: e.tensor_scalar(out=scT[:].rearrange("p a b -> p (a b)"),
                                           in0=scT[:].rearrange("p a b -> p (a b)"), scalar1=0.5, scalar2=None,
                                           op0=ALU.mult), reads=["scT1"], writes=["scT"])

    if DEBUG_STOP == 10:
        S.final_wait()
        S.emit(nc, es)
        es.close()
        return nc
    stf = [xt[:].rearrange("p a b -> p (a b)"), ring[:].rearrange("p a b -> p (a b)").bitcast(F32)[:, 0:8192]]
    stb = [hT[:].rearrange("p a b -> p (a b)"), ya[:].rearrange("p a b -> p (a b)")]
    cvt_i = [0]
    cast_engs = ["act", "dve", "pool"]

    def convert(src_ap_3d, dst_ap, relayout4):
        i = cvt_i[0]
        cvt_i[0] += 1
        b = i % 2
        f = stf[b]
        o = stb[b]
        dma("sp", ("cvl", b), f.rearrange("p (k n) -> p k n", k=16), src_ap_3d,
            writes=[("stf", b)])
        eng = cast_engs[i % 3]
        if relayout4:
            oin = f.rearrange("p (k i j) -> p i k j", k=16, i=4)
            oout = o.rearrange("p (i k j) -> p i k j", i=4, k=16)
        else:
            oin = f
            oout = o
        if eng == "act":
            S.add("act", lambda e: e.activation(out=oout, in_=oin, func=AF.Copy),
                  reads=[("stf", b)], writes=[("stb", b)])
        else:
            S.add(eng, lambda e: e.tensor_copy(out=oout, in_=oin), reads=[("stf", b)], writes=[("stb", b)])
        dma("sp", ("cvs", b), dst_ap, (o.rearrange("p (i f) -> p i f", i=4) if relayout4 else o), reads=[("stb", b)])

    for l in range(2):
        for blk in range(24):
            convert(w_in[l][:, blk * 512:(blk + 1) * 512].rearrange("(k p) n -> p k n", p=128),
                    win_s[l, blk * 4:(blk + 1) * 4].rearrange("i p f -> p i f"), True)
        for (src, dst) in ((w_pa, pa_s), (w_pb, pb_s)):
            for blk in range(4):
                convert(src[l][:, blk * 512:(blk + 1) * 512].rearrange("(k p) n -> p k n", p=128),
                        dst[l, blk * 4:(blk + 1) * 4].rearrange("i p f -> p i f"), True)
        for nb in range(4):
            convert(w_out[l][:, nb * 512:(nb + 1) * 512].rearrange("(k p) n -> p k n", p=128),
                    wo_s[l, nb], False)
        convert(pool_w[l].rearrange("(k p) n -> p k n", p=128), pw_s[l], False)
    for l in range(2):
        for (src, dst) in ((lru_wa, la_s), (lru_wx, lx_s)):
            i = cvt_i[0]
            cvt_i[0] += 1
            b = i % 2
            f = stf[b][:, 0:2048]
            o = stb[b][:, 0:2048]
            dma("sp", ("cvl", b), f.rearrange("p (k n) -> p k n", k=16),
                src[l].rearrange("(k p) n -> p k n", p=128), writes=[("stf", b)])
            S.add("dve", lambda e, f=f, o=o: e.tensor_copy(out=o, in_=f), reads=[("stf", b)], writes=[("stb", b)])
            dma("sp", ("cvs", b), dst[l], o, reads=[("stb", b)])

    if DEBUG_STOP == 11:
        S.final_wait()
        S.emit(nc, es)
        es.close()
        return nc
    for l in range(2):
        for blk in range(12):
            i = cvt_i[0]
            cvt_i[0] += 1
            b = i % 2
            f3 = stf[b].rearrange("p (k n) -> p k n", k=16)
            dma("sp", ("cvl", b), f3, w_ada[l][:, blk * 512:(blk + 1) * 512].rearrange("(k p) n -> p k n", p=128),
                writes=[("stf", b)])
            if blk < 8:
                for j in range(4):
                    cg = blk * 4 + j
                    for k in range(16):
                        S.add("pe", lambda e, f3=f3, j=j, k=k: e.matmul(
                            ps[0][:, j * 8:j * 8 + 5], lhsT=f3[:, k, j * 128:(j + 1) * 128], rhs=scT[:, k, :],
                            start=(k == 0), stop=(k == 15)),
                            reads=[("stf", b), "scT"], writes=[PS(0)])
                    S.add("dve", lambda e, j=j, cg=cg, l=l: e.tensor_scalar(
                        out=modT[:, cg, :], in0=ps[0][:, j * 8:j * 8 + 5], scalar1=badaT[:, l, cg:cg + 1],
                        scalar2=None, op0=ALU.add), reads=[PS(0), "badaT"], writes=[("modT", cg)])
                    if cg < 16:
                        S.add("dve", lambda e, cg=cg, l=l: e.tensor_copy(out=shf[:, l, cg, :], in_=modT[:, cg, :]),
                              reads=[("modT", cg)], writes=["shf"])
                    else:
                        c = cg - 16
                        S.add("dve", lambda e, cg=cg, c=c, l=l: e.tensor_scalar(
                            out=sc1[:, l, c, :], in0=modT[:, cg, :], scalar1=1.0, scalar2=vecs[:, l, 0, c:c + 1],
                            op0=ALU.add, op1=ALU.mult), reads=[("modT", cg), "vecs"], writes=["sc1"])
            else:
                nb = blk - 8
                for k in range(16):
                    S.add("pe", lambda e, f3=f3, k=k: e.matmul(
                        ps[1][0:5, :], lhsT=scT[:, k, :], rhs=f3[:, k, :], start=(k == 0), stop=(k == 15)),
                        reads=[("stf", b), "scT"], writes=[PS(1)])
                S.add("dve", lambda e, nb=nb, l=l: e.tensor_tensor(
                    out=grow[0:5, nb * 512:(nb + 1) * 512], in0=ps[1][0:5, :],
                    in1=bgate[:, l, nb * 512:(nb + 1) * 512], op=ALU.add),
                    reads=[PS(1), "bgate"], writes=[("grow", nb)])
        for cfg in range(3):
            o = stb[cfg % 2][:, 0:4096].bitcast(F32)
            for nb in range(4):
                S.add("pe", lambda e, cfg=cfg, nb=nb: e.matmul(
                    ps[2 + nb][:, :], lhsT=sel[:, cfg, :], rhs=grow[0:5, nb * 512:(nb + 1) * 512],
                    start=True, stop=True), reads=["sel", ("grow", nb)], writes=[PS(2 + nb)])
                S.add("act", lambda e, o=o, nb=nb: e.activation(
                    out=o[:, nb * 512:(nb + 1) * 512], in_=ps[2 + nb][:, :], func=AF.Copy, scale=0.5),
                    reads=[PS(2 + nb)], writes=[("stb", cfg % 2)])
            dma("sp", ("cvs", cfg % 2), gbc_s[l, cfg], o, reads=[("stb", cfg % 2)])

    if DEBUG_STOP == 12:
        S.final_wait()
        S.emit(nc, es)
        es.close()
        return nc
    S.fence()
    if DEBUG_STOP == 13:
        S.final_wait()
        S.emit(nc, es)
        es.close()
        return nc

    ring_pos = [0]

    def ring_load(src_ap, nslots=1):
        p = ring_pos[0]
        if nslots == 4:
            p = ((p + 3) // 4) * 4
        if p + nslots > NS:
            p = 0
        ring_pos[0] = (p + nslots) % NS
        keys = [("ring", p + i) for i in range(nslots)]
        out = ring[:, p:p + nslots, :].rearrange("p a b -> p (a b)")
        dma("sp", ("ring", p), out, src_ap, writes=keys)
        return p, keys

    xstage = [ya[:].rearrange("p a b -> p (a b)").bitcast(F32), yb[:].rearrange("p a b -> p (a b)").bitcast(F32)]
    xstage_keys = [[("ya", k) for k in range(16)], [("yb", k) for k in range(16)]]

    def layer(l, N, nseg, slots, gcfgs, first_tile, prefetch=None, xsrc=False, after_norm=None):
        L = N // nseg
        nsub = N // 128
        dma("sp", "lw", lwa[:].rearrange("p a b -> p (a b)"), la_s[l], writes=["lwa"])
        dma("sp", "lw", lwx[:].rearrange("p a b -> p (a b)"), lx_s[l], writes=["lwx"])
        gtiles = [bc[:], sgb[:].rearrange("p a b -> p (a b)")]
        gkeys = [["bc"], [("sgb", i) for i in range(4)]]
        if xsrc:
            xin = [xstage[j // 2][:, (j % 2) * D:(j % 2 + 1) * D] for j in range(nsub)]
            xink = [xstage_keys[j // 2] for j in range(nsub)]
        else:
            xin = [xt[:, j, :] for j in range(nsub)]
            xink = [[("xt", j)] for j in range(nsub)]
        for j in range(nsub):
            S.add("act", lambda e, j=j: e.activation(out=xn[j % 2][:], in_=xin[j], func=AF.Square,
                                                      accum_out=ss[:, j:j + 1]),
                  reads=xink[j], writes=[("xn", 0), ("ss", j)])
        S.add("act", lambda e: e.activation(out=sd[:, 0:nsub], in_=ss[:, 0:nsub], func=AF.Sqrt,
                                            scale=1.0 / D, bias=eps_t[:, 0:1]),
              reads=[("ss", j) for j in range(nsub)] + ["eps"], writes=["sd"])
        S.add("dve", lambda e: e.reciprocal(out=rstd[:, 0:nsub], in_=sd[:, 0:nsub]), reads=["sd"], writes=["rstd"])
        for j in range(nsub):
            if DEBUG_STOP == 30:
                raise StopBuild()
            S.add("act", lambda e, j=j: e.activation(out=xn[j % 2][:], in_=xin[j], func=AF.Identity,
                                                      scale=rstd[:, j:j + 1]),
                  reads=xink[j] + ["rstd"], writes=[("xn", 0)])
            for k in range(16):
                bank = 6 + k // 8
                pv = ps[bank][:].bitcast(BF16)[:, (k % 8) * 128:(k % 8 + 1) * 128]
                S.add("pe", lambda e, pv=pv, j=j, k=k: e.transpose(out=pv, in_=xn[j % 2][:, k * 128:(k + 1) * 128],
                                                                    identity=identb[:]),
                      reads=[("xn", 0), "identb"], writes=[PS(bank)])
            if DEBUG_STOP == 31:
                raise StopBuild()
            for k in range(16):
                bank = 6 + k // 8
                if nseg == 1:
                    parts = [(0, 128, slots[0])]
                else:
                    parts = [((s * L) % 128, L, slots[s]) for s in range(nseg) if (s * L) // 128 == j]
                for (o0, ln, slot) in parts:
                    pv = ps[bank][:].bitcast(BF16)[:, (k % 8) * 128 + o0:(k % 8) * 128 + o0 + ln]
                    engn = "dve" if (k < 8) else "act"
                    if engn == "dve":
                        S.add("dve", lambda e, pv=pv, j=j, k=k, o0=o0, ln=ln, slot=slot: e.tensor_scalar(
                            out=hT[:, k, j * 128 + o0:j * 128 + o0 + ln], in0=pv,
                            scalar1=sc1[:, l, k, slot:slot + 1], scalar2=shf[:, l, k, slot:slot + 1],
                            op0=ALU.mult, op1=ALU.add), reads=[PS(bank)], writes=[("hT", k)])
                    else:
                        S.add("act", lambda e, pv=pv, j=j, k=k, o0=o0, ln=ln, slot=slot: e.activation(
                            out=hT[:, k, j * 128 + o0:j * 128 + o0 + ln], in_=pv, func=AF.Identity,
                            scale=sc1[:, l, k, slot:slot + 1], bias=shf[:, l, k, slot:slot + 1]),
                            reads=[PS(bank)], writes=[("hT", k)])
        if DEBUG_STOP == 20:
            raise StopBuild()
        hkeys = [("hT", k) for k in range(16)]
        if after_norm is not None:
            after_norm()

        def proj(bank, wslot, wkeys, act, akeys):
            for k in range(16):
                S.add("pe", lambda e, k=k: e.matmul(ps[bank][:, 0:N], lhsT=ring[:, wslot, k * 128:(k + 1) * 128],
                                                    rhs=act[:, k, 0:N], start=(k == 0), stop=(k == 15)),
                      reads=wkeys + [akeys[k]], writes=[PS(bank)])

        def seg3(ap2d, H):
            return ap2d[:, 0:nseg * (H + L)].rearrange("p (s l) -> p s l", s=nseg)

        s0 = slots[0]
        BXA, BGA, BXB, BGB, BR, BI = 0, 1, 2, 3, 4, 5

        def tail(c):
            q = c % 2
            cv = tmp["cv"][q][:, 0:N]
            tr = tmp["tr"][0][:, 0:N]
            ti = tmp["ti"][0][:, 0:N]
            a2 = tmp["a2"][0][:, 0:N]
            hh = tmp["h"][0][:, 0:N]
            tg = tmp["tg"][q][:, 0:N]
            S.add("act", lambda e: e.activation(out=tr, in_=ps[BR][:, 0:N], func=AF.Tanh, scale=0.5,
                                                bias=bah[:, l, c:c + 1]),
                  reads=[PS(BR), "bah"], writes=["tr"])
            S.add("act", lambda e: e.activation(out=ti, in_=ps[BI][:, 0:N], func=AF.Tanh, scale=0.5,
                                                bias=bxh[:, l, c:c + 1]),
                  reads=[PS(BI), "bxh"], writes=["ti"])
            S.add("dve", lambda e: e.tensor_scalar(out=tr, in0=tr, scalar1=chh[:, l, c:c + 1],
                                                   scalar2=chh[:, l, c:c + 1], op0=ALU.mult, op1=ALU.add),
                  reads=["tr", "chh"], writes=["tr"])
            S.add("act", lambda e: e.activation(out=a2, in_=tr, func=AF.Exp, scale=2.0),
                  reads=["tr"], writes=["a2"])
            S.add("act", lambda e: e.activation(out=tr, in_=tr, func=AF.Exp), reads=["tr"], writes=["tr"])
            S.add("dve", lambda e: e.tensor_scalar(out=a2, in0=a2, scalar1=1.0, scalar2=None, op0=ALU.min),
                  reads=["a2"], writes=["a2"])
            S.add("act", lambda e: e.activation(out=a2, in_=a2, func=AF.Sqrt, scale=-0.25, bias=q25[:, 0:1]),
                  reads=["a2", "q25"], writes=["a2"])
            S.add("dve", lambda e: e.scalar_tensor_tensor(out=ti, in0=ti, scalar=1.0, in1=cv, op0=ALU.add,
                                                          op1=ALU.mult),
                  reads=["ti", ("cv", q)], writes=["ti"])
            S.add("dve", lambda e: e.tensor_tensor(out=ti, in0=ti, in1=a2, op=ALU.mult),
                  reads=["ti", "a2"], writes=["ti"])
            for s in range(nseg):
                S.add("dve", lambda e, s=s: e.tensor_tensor_scan(
                    out=hh[:, s * L:(s + 1) * L], data0=tr[:, s * L:(s + 1) * L], data1=ti[:, s * L:(s + 1) * L],
                    initial=Slru[:, l, slots[s], c:c + 1], op0=ALU.mult, op1=ALU.add),
                    reads=["tr", "ti", ("Slru", l, c)], writes=["h"])
            h3 = hh.rearrange("p (s l) -> p s l", s=nseg)
            S.add("dve", lambda e: e.tensor_copy(out=Slru[:, l, s0:s0 + nseg, c:c + 1], in_=h3[:, :, L - 1:L]),
                  reads=["h"], writes=[("Slru", l, c)])
            S.add("dve", lambda e: e.scalar_tensor_tensor(out=ya[:, c, 0:N], in0=tg, scalar=0.5, in1=hh,
                                                          op0=ALU.mult, op1=ALU.mult),
                  reads=[("tg", q), "h"], writes=[("ya", c)])

        def gates(c):
            q = c % 2
            S.add("pe", lambda e: e.matmul(ps[BR][:, 0:N], lhsT=lwa[:, c, :], rhs=cvb[q][:, 0:N],
                                           start=True, stop=True),
                  reads=["lwa", ("cvb", q)], writes=[PS(BR)])
            S.add("pe", lambda e: e.matmul(ps[BI][:, 0:N], lhsT=lwx[:, c, :], rhs=cvb[q][:, 0:N],
                                           start=True, stop=True),
                  reads=["lwx", ("cvb", q)], writes=[PS(BI)])

        def pool_mm(g, pw_slot, pw_keys):
            pwv = ring[:, pw_slot, :].rearrange("p (a b) -> p a b", a=4)
            for ec in range(4):
                bo = 6 + ec % 2
                for c2 in range(4):
                    S.add("pe", lambda e, bo=bo, c2=c2, ec=ec: e.matmul(
                        ps[bo][:, 0:N], lhsT=pwv[:, c2, ec * 128:(ec + 1) * 128], rhs=dbf[:, c2, 0:N],
                        start=(c2 == 0), stop=(c2 == 3)), reads=pw_keys + [("dbf", c2)], writes=[PS(bo)])
                oc = g * 4 + ec
                S.add("dve", lambda e, bo=bo, oc=oc, ec=ec: e.scalar_tensor_tensor(
                    out=yb[:, oc, 0:N], in0=ps[bo][:, 0:N], scalar=psh[:, l, oc:oc + 1],
                    in1=sgb[:, ec, 0:N], op0=ALU.mult, op1=ALU.mult),
                    reads=[PS(bo), "psh", ("sgb", ec)], writes=[("yb", oc)])

        for c in range(16):
            q = c % 2
            g = c // 4
            cc = c % 4
            if cc == 3:
                pend_pw = (g,) + ring_load(pw_s[l][:, g * 2048:(g + 1) * 2048])
            wx_slot, wx_keys = ring_load(win_s[l, c])
            wg_slot, wg_keys = ring_load(win_s[l, 16 + c])
            proj(BXA, wx_slot, wx_keys, hT, hkeys)
            proj(BGA, wg_slot, wg_keys, hT, hkeys)
            if c > 0:
                gates(c - 1)
            if cc == 0 and c > 0:
                pool_mm(*pend_pw)
            wb_slot, wb_keys = ring_load(win_s[l, 32 + c])
            wh_slot, wh_keys = ring_load(win_s[l, 48 + c])
            proj(BXB, wb_slot, wb_keys, hT, hkeys)
            proj(BGB, wh_slot, wh_keys, hT, hkeys)
            e3 = seg3(ext[0], 3)
            S.add("pool", lambda e, e3=e3, c=c: e.tensor_copy(out=e3[:, :, 0:3],
                                                             in_=Sconv[:, l, s0:s0 + nseg, c, :]),
                  reads=[("Sconv", l, c)], writes=["exth"])
            S.add("act", lambda e, e3=e3: e.activation(
                out=e3[:, :, 3:3 + L], in_=ps[BXA][:, 0:N].rearrange("p (s l) -> p s l", s=nseg), func=AF.Copy),
                reads=[PS(BXA)], writes=["ext"])
            cv = tmp["cv"][q][:, 0:N]
            cv3 = cv.rearrange("p (s l) -> p s l", s=nseg)
            S.add("act", lambda e, cv=cv, c=c: e.activation(
                out=cv, in_=ps[BXA][:, 0:N], func=AF.Identity, scale=vecs[:, l, 4, c:c + 1],
                bias=vecs[:, l, 5, c:c + 1]), reads=[PS(BXA), "vecs"], writes=[("cv", q)])
            S.add("pool", lambda e, e3=e3, c=c: e.tensor_copy(out=Sconv[:, l, s0:s0 + nseg, c, :],
                                                             in_=e3[:, :, L:L + 3]),
                  reads=["ext", "exth"], writes=[("Sconv", l, c)])
            for kk in range(3):
                S.add("dve", lambda e, cv3=cv3, e3=e3, kk=kk, c=c: e.scalar_tensor_tensor(
                    out=cv3, in0=e3[:, :, kk:kk + L], scalar=vecs[:, l, 1 + kk, c:c + 1], in1=cv3,
                    op0=ALU.mult, op1=ALU.add), reads=["ext", "exth", ("cv", q)], writes=[("cv", q)])
            tg = tmp["tg"][q][:, 0:N]
            S.add("act", lambda e, tg=tg: e.activation(out=tg, in_=ps[BGA][:, 0:N], func=AF.Tanh, scale=0.5),
                  reads=[PS(BGA)], writes=[("tg", q)])
            S.add("act", lambda e, cv=cv, q=q: e.activation(out=cvb[q][:, 0:N], in_=cv, func=AF.Copy),
                  reads=[("cv", q)], writes=[("cvb", q)])
            S.add("dve", lambda e, tg=tg: e.scalar_tensor_tensor(
                out=tg, in0=tg, scalar=1.0, in1=ps[BGA][:, 0:N], op0=ALU.add, op1=ALU.mult),
                reads=[("tg", q), PS(BGA)], writes=[("tg", q)])
            p3 = seg3(pext[0], 15)
            S.add("pool", lambda e, p3=p3, c=c: e.tensor_copy(out=p3[:, :, 0:15],
                                                             in_=Spool[:, l, s0:s0 + nseg, c, :]),
                  reads=[("Spool", l, c)], writes=["pexth"])
            S.add("act", lambda e, p3=p3: e.activation(
                out=p3[:, :, 15:15 + L], in_=ps[BXB][:, 0:N].rearrange("p (s l) -> p s l", s=nseg), func=AF.Copy),
                reads=[PS(BXB)], writes=["pext"])
            S.add("pool", lambda e, p3=p3, c=c: e.tensor_copy(out=Spool[:, l, s0:s0 + nseg, c, :],
                                                             in_=p3[:, :, L:L + 15]),
                  reads=["pext", "pexth"], writes=[("Spool", l, c)])
            A3 = seg3(pA, 15)
            B3 = seg3(pB, 15)
            src = p3
            srck = ["pext", "pexth"]
            bufs = [(A3, "pA"), (B3, "pB")]
            sh = 1
            lo = 0
            for step in range(g + 1):
                dst, dk = bufs[step % 2]
                lo2 = lo + sh
                S.add("pool", lambda e, dst=dst, src=src, lo2=lo2, sh=sh: e.tensor_tensor(
                    out=dst[:, :, lo2:15 + L], in0=src[:, :, lo2:15 + L], in1=src[:, :, lo2 - sh:15 + L - sh],
                    op=ALU.add), reads=srck, writes=[dk])
                src, srck = dst, [dk]
                lo = lo2
                sh *= 2
            wwin = float(2 ** (g + 1))
            d3 = dbf[:, cc, 0:N].rearrange("p (s l) -> p s l", s=nseg)
            S.add("dve", lambda e, src=src, p3=p3, d3=d3, wwin=wwin: e.scalar_tensor_tensor(
                out=d3, in0=src[:, :, 15:15 + L], scalar=1.0 / wwin, in1=p3[:, :, 15:15 + L],
                op0=ALU.mult, op1=ALU.subtract), reads=srck + ["pext"], writes=[("dbf", cc)])
            if first_tile:
                S.add("dve", lambda e, src=src, g=g: e.tensor_tensor(
                    out=small[:, 0:16], in0=src[:, 0, 15:31], in1=invc[:, g, :], op=ALU.mult),
                    reads=srck + ["invc"], writes=["small16"])
                S.add("dve", lambda e, p3=p3, cc=cc: e.tensor_tensor(
                    out=dbf[:, cc, 0:16], in0=small[:, 0:16], in1=p3[:, 0, 15:31], op=ALU.subtract),
                    reads=["small16", "pext"], writes=[("dbf", cc)])
            tgb = tmp["tgb"][0][:, 0:N]
            S.add("act", lambda e, tgb=tgb: e.activation(out=tgb, in_=ps[BGB][:, 0:N], func=AF.Tanh, scale=0.5),
                  reads=[PS(BGB)], writes=["tgb"])
            S.add("dve", lambda e, tgb=tgb, cc=cc: e.scalar_tensor_tensor(
                out=sgb[:, cc, 0:N], in0=tgb, scalar=1.0, in1=ps[BGB][:, 0:N], op0=ALU.add, op1=ALU.mult),
                reads=["tgb", PS(BGB)], writes=[("sgb", cc)])
            if c > 0:
                tail(c - 1)
        c0_s3, c0_k3 = ring_load(win_s[l, 64])
        c0_s4, c0_k4 = ring_load(win_s[l, 80])
        proj(2, c0_s3, c0_k3, hT, hkeys)
        proj(3, c0_s4, c0_k4, hT, hkeys)
        pool_mm(*pend_pw)
        gates(15)
        tail(15)
        if DEBUG_STOP == 22:
            raise StopBuild()
        yakeys = [("ya", k) for k in range(16)]
        ybkeys = [("yb", k) for k in range(16)]
        for c in range(16):
            base = 0 if c % 2 == 0 else 4
            if c > 0:
                s3, k3 = ring_load(win_s[l, 64 + c])
                s4, k4 = ring_load(win_s[l, 80 + c])
            s2, k2 = ring_load(pb_s[l, c])
            s1, k1 = ring_load(pa_s[l, c])
            if c > 0:
                proj(base + 2, s3, k3, hT, hkeys)
                proj(base + 3, s4, k4, hT, hkeys)
            proj(base + 1, s2, k2, yb, ybkeys)
            proj(base + 0, s1, k1, ya, yakeys)
            q = 0
            t1 = tmp["tr"][q][:, 0:N]
            t2 = tmp["ti"][q][:, 0:N]
            S.add("act", lambda e, t1=t1, base=base: e.activation(out=t1, in_=ps[base + 2][:, 0:N], func=AF.Tanh,
                                                                  scale=0.5),
                  reads=[PS(base + 2)], writes=["tr"])
            S.add("act", lambda e, t2=t2, base=base: e.activation(out=t2, in_=ps[base + 3][:, 0:N], func=AF.Tanh,
                                                                  scale=0.5),
                  reads=[PS(base + 3)], writes=["ti"])
            S.add("dve", lambda e, t1=t1, base=base: e.scalar_tensor_tensor(
                out=t1, in0=t1, scalar=1.0, in1=ps[base + 0][:, 0:N], op0=ALU.add, op1=ALU.mult),
                reads=["tr", PS(base + 0)], writes=["tr"])
            S.add("dve", lambda e, t2=t2, base=base: e.scalar_tensor_tensor(
                out=t2, in0=t2, scalar=1.0, in1=ps[base + 1][:, 0:N], op0=ALU.add, op1=ALU.mult),
                reads=["ti", PS(base + 1)], writes=["ti"])
            S.add("dve", lambda e, t1=t1, t2=t2, c=c: e.tensor_tensor(out=mT[:, c, 0:N], in0=t1, in1=t2,
                                                                      op=ALU.add),
                  reads=["tr", "ti"], writes=[("mT", c)])
        if DEBUG_STOP == 23:
            raise StopBuild()
        mkeys = [("mT", k) for k in range(16)]
        bi_ = 0
        for gi, cfg in enumerate(gcfgs):
            S.add("pool", lambda e, gi=gi, cfg=cfg: e.dma_start(out=gtiles[gi], in_=gbc_s[l, cfg]),
                  writes=gkeys[gi], dma=("gbc", gi))
        for nb in range(4):
            wslot, wkeys = ring_load(wo_s[l, nb], nslots=4)
            if prefetch is not None and nb == 1:
                src_rows, nsub_n = prefetch
                for j in range(nsub_n):
                    dma("sp", "xin", xstage[j // 2][:, (j % 2) * D:(j % 2 + 1) * D],
                        src_rows[j * 128:(j + 1) * 128, :], writes=xstage_keys[j // 2])
            wv = ring[:, wslot:wslot + 4, :].rearrange("p a b -> p (a b)").rearrange("p (k n) -> p k n", k=16)
            for j in range(nsub):
                bank = bi_ % 8
                bi_ += 1
                for k in range(16):
                    S.add("pe", lambda e, bank=bank, j=j, k=k, wv=wv: e.matmul(
                        ps[bank][:, :], lhsT=mT[:, k, j * 128:(j + 1) * 128], rhs=wv[:, k, :],
                        start=(k == 0), stop=(k == 15)), reads=wkeys + [mkeys[k]], writes=[PS(bank)])
                gi = 0 if len(gcfgs) == 1 else j
                rs = tmp["rs"][0]
                S.add("dve", lambda e, bank=bank, rs=rs, gi=gi, nb=nb: e.tensor_tensor(
                    out=rs[:], in0=ps[bank][:, :], in1=gtiles[gi][:, nb * 512:(nb + 1) * 512], op=ALU.mult),
                    reads=[PS(bank)] + gkeys[gi], writes=[("rs", 0)])
                S.add("dve", lambda e, rs=rs, j=j, nb=nb: e.tensor_tensor(
                    out=xt[:, j, nb * 512:(nb + 1) * 512], in0=xt[:, j, nb * 512:(nb + 1) * 512], in1=rs[:],
                    op=ALU.add), reads=[("rs", 0), ("xt", j)], writes=[("xt", j)])

    eps_t = sb("eps_t", [128, 1])
    S.add("dve", lambda e: e.memset(eps_t[:], EPS), writes=["eps"])
    S.add("dve", lambda e: e.memset(q25[:], 0.25), writes=["q25"])

    def final_norm(nsub, dst_rows):
        for j in range(nsub):
            S.add("act", lambda e, j=j: e.activation(out=xn[j % 2][:], in_=xt[:, j, :], func=AF.Square,
                                                      accum_out=ss[:, j:j + 1]),
                  reads=[("xt", j)], writes=[("xn", 0), ("ss", j)])
        S.add("act", lambda e: e.activation(out=sd[:, 0:nsub], in_=ss[:, 0:nsub], func=AF.Sqrt,
                                            scale=1.0 / D, bias=eps_t[:, 0:1]),
              reads=[("ss", j) for j in range(nsub)] + ["eps"], writes=["sd"])
        S.add("dve", lambda e: e.reciprocal(out=rstd[:, 0:nsub], in_=sd[:, 0:nsub]), reads=["sd"], writes=["rstd"])
        S.add("pool", lambda e: e.dma_start(out=bc[:], in_=fgb_d), writes=["bc"], dma=("gbc", 0))
        for j in range(nsub):
            S.add("dve", lambda e, j=j: e.scalar_tensor_tensor(
                out=xt[:, j, :], in0=xt[:, j, :], scalar=rstd[:, j:j + 1], in1=bc[:], op0=ALU.mult, op1=ALU.mult),
                reads=[("xt", j), "rstd", "bc"], writes=[("xt", j)])
            S.add("pool", lambda e, j=j: e.dma_start(out=dst_rows[j * 128:(j + 1) * 128, :], in_=xt[:, j, :]),
                  reads=[("xt", j)], dma="yout")

    try:
        if DEBUG_STOP >= 20:
            dma("sp", "xin", xt[:, 0:4, :], xp[0:512, :].rearrange("(j p) d -> p j d", p=128),
                writes=[("xt", j) for j in range(4)])
            layer(0, 512, 1, [0], [0], True)
    except StopBuild:
        S.final_wait()
        S.emit(nc, es)
        es.close()
        return nc
    def x_from_stage(nsub_n):
        for j in range(nsub_n):
            S.add("pool", lambda e, j=j: e.tensor_copy(out=xt[:, j, :],
                                                       in_=xstage[j // 2][:, (j % 2) * D:(j % 2 + 1) * D]),
                  reads=xstage_keys[j // 2], writes=[("xt", j)])

    def boundary(tprev, nsub_n):
        def f():
            final_norm(4, yp[tprev * 512:(tprev + 1) * 512, :])
            x_from_stage(nsub_n)
        return f

    for t in range(NT_P):
        if t == 0:
            dma("sp", "xin", xt[:, 0:4, :], xp[0:512, :].rearrange("(j p) d -> p j d", p=128),
                writes=[("xt", j) for j in range(4)])
        nxt = (xp[(t + 1) * 512:(t + 2) * 512, :], 4) if t + 1 < NT_P else (xs, 2)
        for l in range(2):
            layer(l, 512, 1, [0], [0], t == 0, prefetch=(nxt if l == 1 else None),
                  xsrc=(l == 0 and t > 0), after_norm=(boundary(t - 1, 4) if (l == 0 and t > 0) else None))
            if DEBUG_STOP == 1:
                break
        if DEBUG_STOP == 1:
            S.final_wait()
            S.emit(nc, es)
            es.close()
            return nc
    for l in range(2):
        layer(l, 256, 4, [1, 2, 3, 4], [1, 2], False, xsrc=(l == 0),
              after_norm=(boundary(NT_P - 1, 2) if l == 0 else None))
    final_norm(2, ys)
    S.add("pool", lambda e: e.dma_start(out=oconv_d, in_=Sconv[:].rearrange("p a b c d -> p (a b c d)")),
          reads=[("Sconv", l_, c_) for l_ in range(2) for c_ in range(16)], dma="sout")
    S.add("pool", lambda e: e.dma_start(out=opool_d, in_=Spool[:].rearrange("p a b c d -> p (a b c d)")),
          reads=[("Spool", l_, c_) for l_ in range(2) for c_ in range(16)], dma="sout")
    S.add("pool", lambda e: e.dma_start(out=olru_d, in_=Slru[:].rearrange("p a b c -> p (a b c)")),
          reads=[("Slru", l_, c_) for l_ in range(2) for c_ in range(16)], dma="sout")
    S.final_wait()
    S.emit(nc, es)
    es.close()
    return nc


_NC = None


def _get_nc():
    global _NC
    if _NC is None:
        _NC = build_program()
    return _NC


def _fm(v):
    v = np.asarray(v, np.float32)
    lead = v.shape[:-1]
    r = v.reshape(*lead, NCH, 128)
    return np.ascontiguousarray(np.moveaxis(r, -1, 0))


def kernel(x_prompt, x_sample, c_prompt, c_sample, state_conv, state_lru, state_pool,
           norm_g, w_ada, b_ada, w_in, conv_w, conv_b, lru_wa, lru_ba, lru_wx, lru_bx,
           lru_lam, pool_w, pool_scale, w_proj_a, w_proj_b, w_out, final_g):
    f32 = np.float32
    x_prompt = np.asarray(x_prompt, f32)
    x_sample = np.asarray(x_sample, f32)
    nc = _get_nc()
    vecs = np.zeros((128, 2, 10, NCH), f32)
    for l in range(2):
        rows = [norm_g[l], conv_w[l][0], conv_w[l][1], conv_w[l][2], conv_w[l][3], conv_b[l],
                lru_ba[l], lru_bx[l], lru_lam[l], pool_scale[l]]
        for v, r in enumerate(rows):
            vecs[:, l, v, :] = np.asarray(r, f32).reshape(NCH, 128).T
    b_ada = np.asarray(b_ada, f32)
    badaT = np.ascontiguousarray(b_ada[:, :4096].reshape(2, 32, 128).transpose(2, 0, 1))
    bgate = np.ascontiguousarray(np.broadcast_to(b_ada[None, :, 4096:], (5, 2, D)))
    fgb = np.ascontiguousarray(np.broadcast_to(np.asarray(final_g, f32)[None, :], (128, D)))
    sel = np.zeros((5, 3, 128), f32)
    sel[0, 0, :] = 1.0
    sel[1, 1, :64] = 1.0
    sel[2, 1, 64:] = 1.0
    sel[3, 2, :64] = 1.0
    sel[4, 2, 64:] = 1.0
    invc = np.zeros((128, 4, 16), f32)
    for g, w in enumerate((2, 4, 8, 16)):
        invc[:, g, :] = 1.0 / np.minimum(w, np.arange(16) + 1).astype(f32)
    ident = np.eye(128, dtype=f32)
    shared = {
        "vecs": vecs.reshape(128, -1), "badaT": badaT.reshape(128, -1), "bgate": bgate.reshape(5, -1),
        "fgb": fgb, "sel": sel.reshape(5, -1), "invc": invc.reshape(128, -1), "ident": ident,
        "w_ada": np.asarray(w_ada, f32), "w_in": np.asarray(w_in, f32),
        "lru_wa": np.asarray(lru_wa, f32).reshape(2, D, 128), "lru_wx": np.asarray(lru_wx, f32).reshape(2, D, 128),
        "pool_w": np.asarray(pool_w, f32).reshape(2, D, 512),
        "w_proj_a": np.asarray(w_proj_a, f32), "w_proj_b": np.asarray(w_proj_b, f32), "w_out": np.asarray(w_out, f32),
    }
    zeros_seq = np.zeros((SEQ, D), f32)
    state_conv = np.asarray(state_conv, f32)
    state_pool = np.asarray(state_pool, f32)
    state_lru = np.asarray(state_lru, f32)
    in_maps = []
    for k in range(NCORES):
        m = dict(shared)
        m["xp"] = x_prompt[k] if k < 2 else zeros_seq
        m["xs"] = np.ascontiguousarray(x_sample[4 * k:4 * k + 4].reshape(256, D))
        c5 = np.concatenate([np.asarray(c_prompt, f32)[k % 2][None], np.asarray(c_sample, f32)[4 * k:4 * k + 4]], 0)
        m["c5T"] = np.ascontiguousarray(c5.reshape(5, NCH, 128).transpose(2, 1, 0)).reshape(128, -1)
        sc = state_conv[:, 4 * k:4 * k + 4]
        m["sconvT"] = np.ascontiguousarray(sc.reshape(2, 4, 3, NCH, 128).transpose(4, 0, 1, 3, 2)).reshape(128, -1)
        sp_ = state_pool[:, 4 * k:4 * k + 4]
        m["spoolT"] = np.ascontiguousarray(sp_.reshape(2, 4, 15, NCH, 128).transpose(4, 0, 1, 3, 2)).reshape(128, -1)
        sl = state_lru[:, 4 * k:4 * k + 4]
        m["slruT"] = np.ascontiguousarray(sl.reshape(2, 4, NCH, 128).transpose(3, 0, 1, 2)).reshape(128, -1)
        in_maps.append(m)
    res = run_bass_kernel_spmd(nc, in_maps, core_ids=list(range(NCORES)))
    R = res.results
    y_prompt = np.stack([np.asarray(R[k]["yp"], f32) for k in range(2)], 0)
    y_sample = np.concatenate([np.asarray(R[k]["ys"], f32).reshape(4, 64, D) for k in range(NCORES)], 0)

    def unfm(a, tail):
        a = np.asarray(a, f32).reshape(128, 2, 5, NCH, *tail)
        if tail:
            a = a.transpose(1, 2, 4, 3, 0)
        else:
            a = a.transpose(1, 2, 3, 0)
        return np.ascontiguousarray(a).reshape(2, 5, *tail, D)

    oc = [unfm(R[k]["o_conv"], (3,)) for k in range(NCORES)]
    op = [unfm(R[k]["o_pool"], (15,)) for k in range(NCORES)]
    ol = [unfm(R[k]["o_lru"], ()) for k in range(NCORES)]
    conv_p = np.stack([oc[k][:, 0] for k in range(2)], 1)
    lru_p = np.stack([ol[k][:, 0] for k in range(2)], 1)
    pool_p = np.stack([op[k][:, 0] for k in range(2)], 1)
    conv_s = np.concatenate([oc[k][:, 1:5] for k in range(NCORES)], 1)
    lru_s = np.concatenate([ol[k][:, 1:5] for k in range(NCORES)], 1)
    pool_s = np.concatenate([op[k][:, 1:5] for k in range(NCORES)], 1)
    return (y_prompt, y_sample, conv_p, lru_p, pool_p, conv_s, lru_s, pool_s)
```

```python
from contextlib import ExitStack
import numpy as np
import concourse.bass as bass
import concourse.mybir as mybir
from concourse.bass_utils import run_bass_kernel_spmd

F32 = mybir.dt.float32
BF16 = mybir.dt.bfloat16
AF = mybir.ActivationFunctionType
ALU = mybir.AluOpType

D = 2048
NCH = 16
SEQ = 16384
NT_P = SEQ // 512
DEPTH = 2
NCORES = 8
EPS = 1e-6
SEM_LIMIT = 30000
DEBUG_STOP = 0


class StopBuild(Exception):
    pass


class Op:
    __slots__ = ("eng", "fn", "waits", "signal", "sem", "cnt", "dma", "inc")

    def __init__(self, eng, fn, dma):
        self.eng = eng
        self.fn = fn
        self.waits = []
        self.signal = False
        self.sem = None
        self.cnt = 0
        self.dma = dma
        self.inc = 0


class Sched:
    ENGS = ("pe", "act", "dve", "pool", "sp")

    def __init__(self):
        self.ops = {e: [] for e in self.ENGS}
        self.lastw = {}
        self.readers = {}
        self.dmacount = {}

    def add(self, eng, fn, reads=(), writes=(), dma=None):
        op = Op(eng, fn, dma)
        deps = []
        for k in reads:
            w = self.lastw.get(k)
            if w is not None:
                deps.append((w, True))
            if isinstance(k, tuple) and k[0] == "ps":
                for r in self.readers.get(k, ()):
                    if r.eng != eng:
                        deps.append((r, False))
        for k in writes:
            w = self.lastw.get(k)
            if w is not None:
                deps.append((w, False))
            for r in self.readers.get(k, ()):
                deps.append((r, False))
        for (y, raw) in deps:
            if y is op:
                continue
            if y.dma is not None:
                op.waits.append(("dma", y.dma, 16 * self.dmacount[y.dma]))
            else:
                if y.eng == eng and eng == "pe":
                    continue
                y.signal = True
                op.waits.append(("eng", y))
        for k in writes:
            self.lastw[k] = op
            self.readers[k] = []
        for k in reads:
            self.readers.setdefault(k, []).append(op)
        if dma is not None:
            self.dmacount[dma] = self.dmacount.get(dma, 0) + 1
        self.ops[eng].append(op)
        return op

    def fence(self):
        lasts = []
        for e in self.ENGS:
            real = [o for o in self.ops[e] if o.fn is not None and o.dma is None]
            if real:
                lasts.append(real[-1])
        dm = dict(self.dmacount)
        for e in self.ENGS:
            op = Op(e, None, None)
            for y in lasts:
                if y.dma is None and y.eng != e:
                    y.signal = True
                    op.waits.append(("eng", y))
            for k, c in dm.items():
                op.waits.append(("dma", k, 16 * c))
            self.ops[e].append(op)
        self.lastw = {}
        self.readers = {}

    def final_wait(self):
        op = Op("sp", None, None)
        for k, c in self.dmacount.items():
            op.waits.append(("dma", k, 16 * c))
        for e in self.ENGS:
            if e == "sp":
                continue
            real = [o for o in self.ops[e] if o.fn is not None and o.dma is None]
            if real:
                real[-1].signal = True
                op.waits.append(("eng", real[-1]))
        self.ops["sp"].append(op)

    def emit(self, nc, es):
        engsems = {}
        for e in self.ENGS:
            cur = None
            cnt = 0
            n = 0
            for op in self.ops[e]:
                if op.dma is not None or not op.signal:
                    continue
                if cur is None or cnt >= SEM_LIMIT:
                    cur = es.enter_context(nc.semaphore(f"s_{e}_{n}"))
                    n += 1
                    cnt = 0
                cnt += 1
                op.sem = cur
                op.cnt = cnt
        dmasems = {k: es.enter_context(nc.semaphore(f"d_{i}")) for i, k in enumerate(self.dmacount)}
        block = es.enter_context(nc.Block())

        def run(ename, eng):
            waited = {}
            for op in self.ops[ename]:
                for w in op.waits:
                    if w[0] == "dma":
                        sem, val = dmasems[w[1]], w[2]
                    else:
                        sem, val = w[1].sem, w[1].cnt
                    key = id(sem)
                    if waited.get(key, 0) >= val:
                        continue
                    waited[key] = val
                    eng.wait_ge(sem, val)
                if op.fn is None:
                    continue
                inst = op.fn(eng)
                if op.dma is not None:
                    inst.then_inc(dmasems[op.dma], 16)
                elif op.signal:
                    inst.then_inc(op.sem, 1)

        @block.tensor
        def _(eng):
            run("pe", eng)

        @block.scalar
        def _(eng):
            run("act", eng)

        @block.vector
        def _(eng):
            run("dve", eng)

        @block.gpsimd
        def _(eng):
            run("pool", eng)

        @block.sync
        def _(eng):
            run("sp", eng)


def build_program():
    nc = bass.Bass("TRN2", target_bir_lowering=False)
    S = Sched()
    es = ExitStack()

    def din(name, shape, dt=F32):
        return nc.dram_tensor(name, list(shape), dt, kind="ExternalInput").ap()

    def dout(name, shape, dt=F32):
        return nc.dram_tensor(name, list(shape), dt, kind="ExternalOutput").ap()

    def dint(name, shape, dt):
        return nc.dram_tensor(name, list(shape), dt, kind="Internal").ap()

    xp = din("xp", [SEQ, D])
    xs = din("xs", [256, D])
    c5T_d = din("c5T", [128, NCH * 5])
    vecs_d = din("vecs", [128, 2 * 10 * NCH])
    badaT_d = din("badaT", [128, 2 * 32])
    bgate_d = din("bgate", [5, 2 * D])
    sconv_d = din("sconvT", [128, 2 * 4 * NCH * 3])
    spool_d = din("spoolT", [128, 2 * 4 * NCH * 15])
    slru_d = din("slruT", [128, 2 * 4 * NCH])
    fgb_d = din("fgb", [128, D])
    sel_d = din("sel", [5, 3 * 128])
    invc_d = din("invc", [128, 4 * 16])
    ident_d = din("ident", [128, 128])
    w_ada = din("w_ada", [2, D, 3 * D])
    w_in = din("w_in", [2, D, 6 * D])
    lru_wa = din("lru_wa", [2, D, 128])
    lru_wx = din("lru_wx", [2, D, 128])
    pool_w = din("pool_w", [2, D, 512])
    w_pa = din("w_proj_a", [2, D, D])
    w_pb = din("w_proj_b", [2, D, D])
    w_out = din("w_out", [2, D, D])

    yp = dout("yp", [SEQ, D])
    ys = dout("ys", [256, D])
    oconv_d = dout("o_conv", [128, 2 * 5 * NCH * 3])
    opool_d = dout("o_pool", [128, 2 * 5 * NCH * 15])
    olru_d = dout("o_lru", [128, 2 * 5 * NCH])

    win_s = dint("win_s", [2, 96, 128, 2048], BF16)
    pa_s = dint("pa_s", [2, 16, 128, 2048], BF16)
    pb_s = dint("pb_s", [2, 16, 128, 2048], BF16)
    wo_s = dint("wo_s", [2, 4, 128, 8192], BF16)
    pw_s = dint("pw_s", [2, 128, 8192], BF16)
    la_s = dint("la_s", [2, 128, 2048], BF16)
    lx_s = dint("lx_s", [2, 128, 2048], BF16)
    gbc_s = dint("gbc_s", [2, 3, 128, D], F32)

    def sb(name, shape, dt=F32):
        return es.enter_context(nc.sbuf_tensor(name, list(shape), dt))

    NS = 8
    xt = sb("xt", [128, 4, D])
    ring = sb("ring", [128, NS, 2048], BF16)
    hT = sb("hT", [128, NCH, 512], BF16)
    ya = sb("ya", [128, NCH, 512], BF16)
    yb = sb("yb", [128, NCH, 512], BF16)
    mT = sb("mT", [128, NCH, 512], BF16)
    lwa = sb("lwa", [128, NCH, 128], BF16)
    lwx = sb("lwx", [128, NCH, 128], BF16)
    xn1 = sb("xn", [128, D], BF16)
    xn = [xn1, xn1]
    bc = sb("bc", [128, D])
    dbf = sb("dbf", [128, 4, 512], BF16)
    sgb = sb("sgb", [128, 4, 512])
    tmpn = ["tr", "ti", "a2", "rs", "h", "tgb"]
    tmp = {n: [sb(f"{n}", [128, 512])] * 2 for n in tmpn}
    tmp["cv"] = [sb(f"cv{i}", [128, 512]) for i in range(2)]
    tmp["tg"] = [sb(f"tg{i}", [128, 512]) for i in range(2)]
    q25 = sb("q25", [128, 1])
    cvb = [sb(f"cvb{i}", [128, 512], BF16) for i in range(2)]
    ext1 = sb("ext", [128, 528])
    ext = [ext1, ext1]
    pext1 = sb("pext", [128, 528])
    pext = [pext1, pext1]
    pA = sb("pA", [128, 528])
    pB = sb("pB", [128, 528])
    identb = sb("identb", [128, 128], BF16)
    vecs = sb("vecs_t", [128, 2, 10, NCH])
    bah = sb("bah", [128, 2, NCH])
    bxh = sb("bxh", [128, 2, NCH])
    chh = sb("chh", [128, 2, NCH])
    psh = sb("psh", [128, 2, NCH])
    sc1 = sb("sc1", [128, 2, NCH, 5])
    shf = sb("shf", [128, 2, NCH, 5])
    Sconv = sb("Sconv", [128, 2, 5, NCH, 3])
    Spool = sb("Spool", [128, 2, 5, NCH, 15])
    Slru = sb("Slru", [128, 2, 5, NCH])
    invc = sb("invc_t", [128, 4, 16])
    ss = sb("ss", [128, 4])
    sd = sb("sd", [128, 4])
    rstd = sb("rstd", [128, 4])
    small = sb("small", [128, 64])
    c5T = tmp["tg"][0][:, 0:80].rearrange("p (a b) -> p a b", a=NCH)
    scT = tmp["tg"][0][:, 80:160].rearrange("p (a b) -> p a b", a=NCH)
    badaT = tmp["tg"][0][:, 160:224].rearrange("p (a b) -> p a b", a=2)
    modT = tmp["tg"][0][:, 224:384].rearrange("p (a b) -> p a b", a=32)
    bgate = yb[:].rearrange("p a b -> p (a b)").bitcast(F32)[0:5, :].rearrange("p (a b) -> p a b", a=2)
    grow = mT[:].rearrange("p a b -> p (a b)").bitcast(F32)[0:5, 0:D]
    sel = tmp["tg"][1][0:5, 0:384].rearrange("p (a b) -> p a b", a=3)
    identf = tmp["cv"][0][:, 0:128]

    ps = [es.enter_context(nc.psum_tensor(f"ps{i}", [128, 512], F32)) for i in range(8)]

    def PS(i):
        return ("ps", i)

    def dma(eng, key, out, in_, reads=(), writes=(), **kw):
        S.add(eng, lambda e: e.dma_start(out=out, in_=in_, **kw), reads=reads, writes=writes, dma=key)

    dma("sp", "c0", vecs[:].rearrange("p a b c -> p (a b c)"), vecs_d, writes=["vecs"])
    dma("sp", "c0", c5T[:].rearrange("p a b -> p (a b)"), c5T_d, writes=["c5T"])
    dma("sp", "c0", badaT[:].rearrange("p a b -> p (a b)"), badaT_d, writes=["badaT"])
    dma("sp", "c0", yb[:].rearrange("p a b -> p (a b)").bitcast(F32)[0:5, :], bgate_d, writes=["bgate"])
    dma("sp", "c0", sel[:].rearrange("p a b -> p (a b)"), sel_d, writes=["sel"])
    dma("sp", "c0", invc[:].rearrange("p a b -> p (a b)"), invc_d, writes=["invc"])
    dma("sp", "c0", identf[:], ident_d, writes=["identf"])
    S.add("dve", lambda e: e.memset(Sconv[:].rearrange("p a b c d -> p (a b c d)"), 0.0), writes=["Sconv0"])
    S.add("dve", lambda e: e.memset(Spool[:].rearrange("p a b c d -> p (a b c d)"), 0.0), writes=["Spool0"])
    S.add("dve", lambda e: e.memset(Slru[:].rearrange("p a b c -> p (a b c)"), 0.0), writes=["Slru0"])
    for l in range(2):
        dma("sp", "c0", Sconv[:, l, 1:5].rearrange("p s c r -> p (s c r)"),
            sconv_d[:, l * 192:(l + 1) * 192], reads=["Sconv0"], writes=[("Sconv", l)])
        dma("sp", "c0", Spool[:, l, 1:5].rearrange("p s c r -> p (s c r)"),
            spool_d[:, l * 960:(l + 1) * 960], reads=["Spool0"], writes=[("Spool", l)])
        dma("sp", "c0", Slru[:, l, 1:5].rearrange("p s c -> p (s c)"),
            slru_d[:, l * 64:(l + 1) * 64], reads=["Slru0"], writes=[("Slru", l)])
    S.add("dve", lambda e: e.tensor_copy(out=identb[:], in_=identf[:]), reads=["identf"], writes=["identb"])
    for l in range(2):
        S.add("dve", lambda e, l=l: e.tensor_scalar(out=bah[:, l, :], in0=vecs[:, l, 6, :], scalar1=0.5,
                                                     scalar2=None, op0=ALU.mult), reads=["vecs"], writes=["bah"])
        S.add("dve", lambda e, l=l: e.tensor_scalar(out=bxh[:, l, :], in0=vecs[:, l, 7, :], scalar1=0.5,
                                                     scalar2=None, op0=ALU.mult), reads=["vecs"], writes=["bxh"])
        S.add("dve", lambda e, l=l: e.tensor_scalar(out=psh[:, l, :], in0=vecs[:, l, 9, :], scalar1=0.5,
                                                     scalar2=None, op0=ALU.mult), reads=["vecs"], writes=["psh"])
        S.add("act", lambda e, l=l: e.activation(out=small[:, l * 16:(l + 1) * 16], in_=vecs[:, l, 8, :],
                                                  func=AF.Exp, scale=-1.0), reads=["vecs"], writes=[("small", l)])
        S.add("act", lambda e, l=l: e.activation(out=small[:, 32 + l * 16:32 + (l + 1) * 16],
                                                  in_=small[:, l * 16:(l + 1) * 16], func=AF.Ln, bias=1.0),
              reads=[("small", l)], writes=[("small2", l)])
        S.add("dve", lambda e, l=l: e.tensor_scalar(out=chh[:, l, :], in0=small[:, 32 + l * 16:32 + (l + 1) * 16],
                                                     scalar1=-4.0, scalar2=None, op0=ALU.mult),
              reads=[("small2", l)], writes=["chh"])
    S.add("act", lambda e: e.activation(out=scT[:].rearrange("p a b -> p (a b)"),
                                        in_=c5T[:].rearrange("p a b -> p (a b)"), func=AF.Tanh, scale=0.5),
          reads=["c5T"], writes=["scT0"])
    S.add("dve", lambda e: e.scalar_tensor_tensor(out=scT[:].rearrange("p a b -> p (a b)"),
                                                  in0=scT[:].rearrange("p a b -> p (a b)"), scalar=1.0,
                                                  in1=c5T[:].rearrange("p a b -> p (a b)"),
                                                  op0=ALU.add, op1=ALU.mult),
          reads=["scT0", "c5T"], writes=["scT1"])
    S.add("dve", lambda e: e.tensor_scalar(out=scT[:].rearrange("p a b -> p (a b)"),
                                           in0=scT[:].rearrange("p a b -> p (a b)"), scalar1=0.5, scalar2=None,
                                           op0=ALU.mult), reads=["scT1"], writes=["scT"])

    if DEBUG_STOP == 10:
        S.final_wait()
        S.emit(nc, es)
        es.close()
        return nc
    stf = [xt[:].rearrange("p a b -> p (a b)"), ring[:].rearrange("p a b -> p (a b)").bitcast(F32)[:, 0:8192]]
    stb = [hT[:].rearrange("p a b -> p (a b)"), ya[:].rearrange("p a b -> p (a b)")]
    cvt_i = [0]
    cast_engs = ["act", "dve", "pool"]

    def convert(src_ap_3d, dst_ap, relayout4):
        i = cvt_i[0]
        cvt_i[0] += 1
        b = i % 2
        f = stf[b]
        o = stb[b]
        dma("sp", ("cvl", b), f.rearrange("p (k n) -> p k n", k=16), src_ap_3d,
            writes=[("stf", b)])
        eng = cast_engs[i % 3]
        if relayout4:
            oin = f.rearrange("p (k i j) -> p i k j", k=16, i=4)
            oout = o.rearrange("p (i k j) -> p i k j", i=4, k=16)
        else:
            oin = f
            oout = o
        if eng == "act":
            S.add("act", lambda e: e.activation(out=oout, in_=oin, func=AF.Copy),
                  reads=[("stf", b)], writes=[("stb", b)])
        else:
            S.add(eng, lambda e: e.tensor_copy(out=oout, in_=oin), reads=[("stf", b)], writes=[("stb", b)])
        dma("sp", ("cvs", b), dst_ap, (o.rearrange("p (i f) -> p i f", i=4) if relayout4 else o), reads=[("stb", b)])

    for l in range(2):
        for blk in range(24):
            convert(w_in[l][:, blk * 512:(blk + 1) * 512].rearrange("(k p) n -> p k n", p=128),
                    win_s[l, blk * 4:(blk + 1) * 4].rearrange("i p f -> p i f"), True)
        for (src, dst) in ((w_pa, pa_s), (w_pb, pb_s)):
            for blk in range(4):
                convert(src[l][:, blk * 512:(blk + 1) * 512].rearrange("(k p) n -> p k n", p=128),
                        dst[l, blk * 4:(blk + 1) * 4].rearrange("i p f -> p i f"), True)
        for nb in range(4):
            convert(w_out[l][:, nb * 512:(nb + 1) * 512].rearrange("(k p) n -> p k n", p=128),
                    wo_s[l, nb], False)
        convert(pool_w[l].rearrange("(k p) n -> p k n", p=128), pw_s[l], False)
    for l in range(2):
        for (src, dst) in ((lru_wa, la_s), (lru_wx, lx_s)):
            i = cvt_i[0]
            cvt_i[0] += 1
            b = i % 2
            f = stf[b][:, 0:2048]
            o = stb[b][:, 0:2048]
            dma("sp", ("cvl", b), f.rearrange("p (k n) -> p k n", k=16),
                src[l].rearrange("(k p) n -> p k n", p=128), writes=[("stf", b)])
            S.add("dve", lambda e, f=f, o=o: e.tensor_copy(out=o, in_=f), reads=[("stf", b)], writes=[("stb", b)])
            dma("sp", ("cvs", b), dst[l], o, reads=[("stb", b)])

    if DEBUG_STOP == 11:
        S.final_wait()
        S.emit(nc, es)
        es.close()
        return nc
    for l in range(2):
        for blk in range(12):
            i = cvt_i[0]
            cvt_i[0] += 1
            b = i % 2
            f3 = stf[b].rearrange("p (k n) -> p k n", k=16)
            dma("sp", ("cvl", b), f3, w_ada[l][:, blk * 512:(blk + 1) * 512].rearrange("(k p) n -> p k n", p=128),
                writes=[("stf", b)])
            if blk < 8:
                for j in range(4):
                    cg = blk * 4 + j
                    for k in range(16):
                        S.add("pe", lambda e, f3=f3, j=j, k=k: e.matmul(
                            ps[0][:, j * 8:j * 8 + 5], lhsT=f3[:, k, j * 128:(j + 1) * 128], rhs=scT[:, k, :],
                            start=(k == 0), stop=(k == 15)),
                            reads=[("stf", b), "scT"], writes=[PS(0)])
                    S.add("dve", lambda e, j=j, cg=cg, l=l: e.tensor_scalar(
                        out=modT[:, cg, :], in0=ps[0][:, j * 8:j * 8 + 5], scalar1=badaT[:, l, cg:cg + 1],
                        scalar2=None, op0=ALU.add), reads=[PS(0), "badaT"], writes=[("modT", cg)])
                    if cg < 16:
                        S.add("dve", lambda e, cg=cg, l=l: e.tensor_copy(out=shf[:, l, cg, :], in_=modT[:, cg, :]),
                              reads=[("modT", cg)], writes=["shf"])
                    else:
                        c = cg - 16
                        S.add("dve", lambda e, cg=cg, c=c, l=l: e.tensor_scalar(
                            out=sc1[:, l, c, :], in0=modT[:, cg, :], scalar1=1.0, scalar2=vecs[:, l, 0, c:c + 1],
                            op0=ALU.add, op1=ALU.mult), reads=[("modT", cg), "vecs"], writes=["sc1"])
            else:
                nb = blk - 8
                for k in range(16):
                    S.add("pe", lambda e, f3=f3, k=k: e.matmul(
                        ps[1][0:5, :], lhsT=scT[:, k, :], rhs=f3[:, k, :], start=(k == 0), stop=(k == 15)),
                        reads=[("stf", b), "scT"], writes=[PS(1)])
                S.add("dve", lambda e, nb=nb, l=l: e.tensor_tensor(
                    out=grow[0:5, nb * 512:(nb + 1) * 512], in0=ps[1][0:5, :],
                    in1=bgate[:, l, nb * 512:(nb + 1) * 512], op=ALU.add),
                    reads=[PS(1), "bgate"], writes=[("grow", nb)])
        for cfg in range(3):
            o = stb[cfg % 2][:, 0:4096].bitcast(F32)
            for nb in range(4):
                S.add("pe", lambda e, cfg=cfg, nb=nb: e.matmul(
                    ps[2 + nb][:, :], lhsT=sel[:, cfg, :], rhs=grow[0:5, nb * 512:(nb + 1) * 512],
                    start=True, stop=True), reads=["sel", ("grow", nb)], writes=[PS(2 + nb)])
                S.add("act", lambda e, o=o, nb=nb: e.activation(
                    out=o[:, nb * 512:(nb + 1) * 512], in_=ps[2 + nb][:, :], func=AF.Copy, scale=0.5),
                    reads=[PS(2 + nb)], writes=[("stb", cfg % 2)])
            dma("sp", ("cvs", cfg % 2), gbc_s[l, cfg], o, reads=[("stb", cfg % 2)])

    if DEBUG_STOP == 12:
        S.final_wait()
        S.emit(nc, es)
        es.close()
        return nc
    S.fence()
    if DEBUG_STOP == 13:
        S.final_wait()
        S.emit(nc, es)
        es.close()
        return nc

    ring_pos = [0]

    def ring_load(src_ap, nslots=1):
        p = ring_pos[0]
        if nslots == 4:
            p = ((p + 3) // 4) * 4
        if p + nslots > NS:
            p = 0
        ring_pos[0] = (p + nslots) % NS
        keys = [("ring", p + i) for i in range(nslots)]
        out = ring[:, p:p + nslots, :].rearrange("p a b -> p (a b)")
        dma("sp", ("ring", p), out, src_ap, writes=keys)
        return p, keys

    xstage = [yb[:].rearrange("p a b -> p (a b)").bitcast(F32), mT[:].rearrange("p a b -> p (a b)").bitcast(F32)]
    xstage_keys = [[("yb", k) for k in range(16)], [("mT", k) for k in range(16)]]

    def layer(l, N, nseg, slots, gcfgs, first_tile, prefetch=None, xsrc=False, after_norm=None, mid_hook=None):
        L = N // nseg
        nsub = N // 128
        dma("sp", "lw", lwa[:].rearrange("p a b -> p (a b)"), la_s[l], writes=["lwa"])
        dma("sp", "lw", lwx[:].rearrange("p a b -> p (a b)"), lx_s[l], writes=["lwx"])
        gtiles = [bc[:], sgb[:].rearrange("p a b -> p (a b)")]
        gkeys = [["bc"], [("sgb", i) for i in range(4)]]
        if xsrc:
            xin = [xstage[j // 2][:, (j % 2) * D:(j % 2 + 1) * D] for j in range(nsub)]
            xink = [xstage_keys[j // 2] for j in range(nsub)]
        else:
            xin = [xt[:, j, :] for j in range(nsub)]
            xink = [[("xt", j)] for j in range(nsub)]
        for j in range(nsub):
            S.add("act", lambda e, j=j: e.activation(out=xn[j % 2][:], in_=xin[j], func=AF.Square,
                                                      accum_out=ss[:, j:j + 1]),
                  reads=xink[j], writes=[("xn", 0), ("ss", j)])
        S.add("act", lambda e: e.activation(out=sd[:, 0:nsub], in_=ss[:, 0:nsub], func=AF.Sqrt,
                                            scale=1.0 / D, bias=eps_t[:, 0:1]),
              reads=[("ss", j) for j in range(nsub)] + ["eps"], writes=["sd"])
        S.add("dve", lambda e: e.reciprocal(out=rstd[:, 0:nsub], in_=sd[:, 0:nsub]), reads=["sd"], writes=["rstd"])
        for j in range(nsub):
            if DEBUG_STOP == 30:
                raise StopBuild()
            S.add("act", lambda e, j=j: e.activation(out=xn[j % 2][:], in_=xin[j], func=AF.Identity,
                                                      scale=rstd[:, j:j + 1]),
                  reads=xink[j] + ["rstd"], writes=[("xn", 0)])
            for k in range(16):
                bank = 6 + k // 8
                pv = ps[bank][:].bitcast(BF16)[:, (k % 8) * 128:(k % 8 + 1) * 128]
                S.add("pe", lambda e, pv=pv, j=j, k=k: e.transpose(out=pv, in_=xn[j % 2][:, k * 128:(k + 1) * 128],
                                                                    identity=identb[:]),
                      reads=[("xn", 0), "identb"], writes=[PS(bank)])
            if DEBUG_STOP == 31:
                raise StopBuild()
            for k in range(16):
                bank = 6 + k // 8
                if nseg == 1:
                    parts = [(0, 128, slots[0])]
                else:
                    parts = [((s * L) % 128, L, slots[s]) for s in range(nseg) if (s * L) // 128 == j]
                for (o0, ln, slot) in parts:
                    pv = ps[bank][:].bitcast(BF16)[:, (k % 8) * 128 + o0:(k % 8) * 128 + o0 + ln]
                    engn = "dve" if (k < 8) else "act"
                    if engn == "dve":
                        S.add("dve", lambda e, pv=pv, j=j, k=k, o0=o0, ln=ln, slot=slot: e.tensor_scalar(
                            out=hT[:, k, j * 128 + o0:j * 128 + o0 + ln], in0=pv,
                            scalar1=sc1[:, l, k, slot:slot + 1], scalar2=shf[:, l, k, slot:slot + 1],
                            op0=ALU.mult, op1=ALU.add), reads=[PS(bank)], writes=[("hT", k)])
                    else:
                        S.add("act", lambda e, pv=pv, j=j, k=k, o0=o0, ln=ln, slot=slot: e.activation(
                            out=hT[:, k, j * 128 + o0:j * 128 + o0 + ln], in_=pv, func=AF.Identity,
                            scale=sc1[:, l, k, slot:slot + 1], bias=shf[:, l, k, slot:slot + 1]),
                            reads=[PS(bank)], writes=[("hT", k)])
        if DEBUG_STOP == 20:
            raise StopBuild()
        hkeys = [("hT", k) for k in range(16)]
        if after_norm is not None:
            after_norm()

        def proj(bank, wslot, wkeys, act, akeys):
            for k in range(16):
                S.add("pe", lambda e, k=k: e.matmul(ps[bank][:, 0:N], lhsT=ring[:, wslot, k * 128:(k + 1) * 128],
                                                    rhs=act[:, k, 0:N], start=(k == 0), stop=(k == 15)),
                      reads=wkeys + [akeys[k]], writes=[PS(bank)])

        def seg3(ap2d, H):
            return ap2d[:, 0:nseg * (H + L)].rearrange("p (s l) -> p s l", s=nseg)

        s0 = slots[0]
        BXA, BGA, BXB, BGB, BR, BI = 0, 1, 2, 3, 4, 5

        def tail(c):
            q = c % 2
            cv = tmp["cv"][q][:, 0:N]
            tr = tmp["tr"][0][:, 0:N]
            ti = tmp["ti"][0][:, 0:N]
            a2 = tmp["a2"][0][:, 0:N]
            hh = tmp["h"][0][:, 0:N]
            tg = tmp["tg"][q][:, 0:N]
            S.add("act", lambda e: e.activation(out=tr, in_=ps[BR][:, 0:N], func=AF.Tanh, scale=0.5,
                                                bias=bah[:, l, c:c + 1]),
                  reads=[PS(BR), "bah"], writes=["tr"])
            S.add("act", lambda e: e.activation(out=ti, in_=ps[BI][:, 0:N], func=AF.Tanh, scale=0.5,
                                                bias=bxh[:, l, c:c + 1]),
                  reads=[PS(BI), "bxh"], writes=["ti"])
            S.add("dve", lambda e: e.tensor_scalar(out=tr, in0=tr, scalar1=chh[:, l, c:c + 1],
                                                   scalar2=chh[:, l, c:c + 1], op0=ALU.mult, op1=ALU.add),
                  reads=["tr", "chh"], writes=["tr"])
            S.add("act", lambda e: e.activation(out=a2, in_=tr, func=AF.Exp, scale=2.0),
                  reads=["tr"], writes=["a2"])
            S.add("act", lambda e: e.activation(out=tr, in_=tr, func=AF.Exp), reads=["tr"], writes=["tr"])
            S.add("dve", lambda e: e.tensor_scalar(out=a2, in0=a2, scalar1=1.0, scalar2=None, op0=ALU.min),
                  reads=["a2"], writes=["a2"])
            S.add("act", lambda e: e.activation(out=a2, in_=a2, func=AF.Sqrt, scale=-0.25, bias=q25[:, 0:1]),
                  reads=["a2", "q25"], writes=["a2"])
            S.add("dve", lambda e: e.scalar_tensor_tensor(out=ti, in0=ti, scalar=1.0, in1=cv, op0=ALU.add,
                                                          op1=ALU.mult),
                  reads=["ti", ("cv", q)], writes=["ti"])
            S.add("dve", lambda e: e.tensor_tensor(out=ti, in0=ti, in1=a2, op=ALU.mult),
                  reads=["ti", "a2"], writes=["ti"])
            for s in range(nseg):
                S.add("dve", lambda e, s=s: e.tensor_tensor_scan(
                    out=hh[:, s * L:(s + 1) * L], data0=tr[:, s * L:(s + 1) * L], data1=ti[:, s * L:(s + 1) * L],
                    initial=Slru[:, l, slots[s], c:c + 1], op0=ALU.mult, op1=ALU.add),
                    reads=["tr", "ti", ("Slru", l, c)], writes=["h"])
            h3 = hh.rearrange("p (s l) -> p s l", s=nseg)
            S.add("dve", lambda e: e.tensor_copy(out=Slru[:, l, s0:s0 + nseg, c:c + 1], in_=h3[:, :, L - 1:L]),
                  reads=["h"], writes=[("Slru", l, c)])
            S.add("dve", lambda e: e.scalar_tensor_tensor(out=ya[:, c, 0:N], in0=tg, scalar=0.5, in1=hh,
                                                          op0=ALU.mult, op1=ALU.mult),
                  reads=[("tg", q), "h"], writes=[("ya", c)])

        def gates(c):
            q = c % 2
            S.add("pe", lambda e: e.matmul(ps[BR][:, 0:N], lhsT=lwa[:, c, :], rhs=cvb[q][:, 0:N],
                                           start=True, stop=True),
                  reads=["lwa", ("cvb", q)], writes=[PS(BR)])
            S.add("pe", lambda e: e.matmul(ps[BI][:, 0:N], lhsT=lwx[:, c, :], rhs=cvb[q][:, 0:N],
                                           start=True, stop=True),
                  reads=["lwx", ("cvb", q)], writes=[PS(BI)])

        def pool_mm(g, pw_slot, pw_keys):
            pwv = ring[:, pw_slot, :].rearrange("p (a b) -> p a b", a=4)
            for ec in range(4):
                bo = 6 + ec % 2
                for c2 in range(4):
                    S.add("pe", lambda e, bo=bo, c2=c2, ec=ec: e.matmul(
                        ps[bo][:, 0:N], lhsT=pwv[:, c2, ec * 128:(ec + 1) * 128], rhs=dbf[:, c2, 0:N],
                        start=(c2 == 0), stop=(c2 == 3)), reads=pw_keys + [("dbf", c2)], writes=[PS(bo)])
                oc = g * 4 + ec
                S.add("dve", lambda e, bo=bo, oc=oc, ec=ec: e.scalar_tensor_tensor(
                    out=yb[:, oc, 0:N], in0=ps[bo][:, 0:N], scalar=psh[:, l, oc:oc + 1],
                    in1=sgb[:, ec, 0:N], op0=ALU.mult, op1=ALU.mult),
                    reads=[PS(bo), "psh", ("sgb", ec)], writes=[("yb", oc)])

        for c in range(16):
            q = c % 2
            g = c // 4
            cc = c % 4
            if cc == 3:
                pend_pw = (g,) + ring_load(pw_s[l][:, g * 2048:(g + 1) * 2048])
            wx_slot, wx_keys = ring_load(win_s[l, c])
            wg_slot, wg_keys = ring_load(win_s[l, 16 + c])
            proj(BXA, wx_slot, wx_keys, hT, hkeys)
            proj(BGA, wg_slot, wg_keys, hT, hkeys)
            if c > 0:
                gates(c - 1)
            if cc == 0 and c > 0:
                pool_mm(*pend_pw)
            wb_slot, wb_keys = ring_load(win_s[l, 32 + c])
            wh_slot, wh_keys = ring_load(win_s[l, 48 + c])
            proj(BXB, wb_slot, wb_keys, hT, hkeys)
            proj(BGB, wh_slot, wh_keys, hT, hkeys)
            e3 = seg3(ext[0], 3)
            S.add("pool", lambda e, e3=e3, c=c: e.tensor_copy(out=e3[:, :, 0:3],
                                                             in_=Sconv[:, l, s0:s0 + nseg, c, :]),
                  reads=[("Sconv", l, c)], writes=["exth"])
            S.add("act", lambda e, e3=e3: e.activation(
                out=e3[:, :, 3:3 + L], in_=ps[BXA][:, 0:N].rearrange("p (s l) -> p s l", s=nseg), func=AF.Copy),
                reads=[PS(BXA)], writes=["ext"])
            cv = tmp["cv"][q][:, 0:N]
            cv3 = cv.rearrange("p (s l) -> p s l", s=nseg)
            S.add("act", lambda e, cv=cv, c=c: e.activation(
                out=cv, in_=ps[BXA][:, 0:N], func=AF.Identity, scale=vecs[:, l, 4, c:c + 1],
                bias=vecs[:, l, 5, c:c + 1]), reads=[PS(BXA), "vecs"], writes=[("cv", q)])
            S.add("pool", lambda e, e3=e3, c=c: e.tensor_copy(out=Sconv[:, l, s0:s0 + nseg, c, :],
                                                             in_=e3[:, :, L:L + 3]),
                  reads=["ext", "exth"], writes=[("Sconv", l, c)])
            for kk in range(3):
                S.add("dve", lambda e, cv3=cv3, e3=e3, kk=kk, c=c: e.scalar_tensor_tensor(
                    out=cv3, in0=e3[:, :, kk:kk + L], scalar=vecs[:, l, 1 + kk, c:c + 1], in1=cv3,
                    op0=ALU.mult, op1=ALU.add), reads=["ext", "exth", ("cv", q)], writes=[("cv", q)])
            tg = tmp["tg"][q][:, 0:N]
            S.add("act", lambda e, tg=tg: e.activation(out=tg, in_=ps[BGA][:, 0:N], func=AF.Tanh, scale=0.5),
                  reads=[PS(BGA)], writes=[("tg", q)])
            S.add("act", lambda e, cv=cv, q=q: e.activation(out=cvb[q][:, 0:N], in_=cv, func=AF.Copy),
                  reads=[("cv", q)], writes=[("cvb", q)])
            S.add("dve", lambda e, tg=tg: e.scalar_tensor_tensor(
                out=tg, in0=tg, scalar=1.0, in1=ps[BGA][:, 0:N], op0=ALU.add, op1=ALU.mult),
                reads=[("tg", q), PS(BGA)], writes=[("tg", q)])
            p3 = seg3(pext[0], 15)
            S.add("pool", lambda e, p3=p3, c=c: e.tensor_copy(out=p3[:, :, 0:15],
                                                             in_=Spool[:, l, s0:s0 + nseg, c, :]),
                  reads=[("Spool", l, c)], writes=["pexth"])
            S.add("act", lambda e, p3=p3: e.activation(
                out=p3[:, :, 15:15 + L], in_=ps[BXB][:, 0:N].rearrange("p (s l) -> p s l", s=nseg), func=AF.Copy),
                reads=[PS(BXB)], writes=["pext"])
            S.add("pool", lambda e, p3=p3, c=c: e.tensor_copy(out=Spool[:, l, s0:s0 + nseg, c, :],
                                                             in_=p3[:, :, L:L + 15]),
                  reads=["pext", "pexth"], writes=[("Spool", l, c)])
            A3 = seg3(pA, 15)
            B3 = seg3(pB, 15)
            src = p3
            srck = ["pext", "pexth"]
            bufs = [(A3, "pA"), (B3, "pB")]
            sh = 1
            lo = 0
            for step in range(g + 1):
                dst, dk = bufs[step % 2]
                lo2 = lo + sh
                S.add("pool", lambda e, dst=dst, src=src, lo2=lo2, sh=sh: e.tensor_tensor(
                    out=dst[:, :, lo2:15 + L], in0=src[:, :, lo2:15 + L], in1=src[:, :, lo2 - sh:15 + L - sh],
                    op=ALU.add), reads=srck, writes=[dk])
                src, srck = dst, [dk]
                lo = lo2
                sh *= 2
            wwin = float(2 ** (g + 1))
            d3 = dbf[:, cc, 0:N].rearrange("p (s l) -> p s l", s=nseg)
            S.add("dve", lambda e, src=src, p3=p3, d3=d3, wwin=wwin: e.scalar_tensor_tensor(
                out=d3, in0=src[:, :, 15:15 + L], scalar=1.0 / wwin, in1=p3[:, :, 15:15 + L],
                op0=ALU.mult, op1=ALU.subtract), reads=srck + ["pext"], writes=[("dbf", cc)])
            if first_tile:
                S.add("dve", lambda e, src=src, g=g: e.tensor_tensor(
                    out=small[:, 0:16], in0=src[:, 0, 15:31], in1=invc[:, g, :], op=ALU.mult),
                    reads=srck + ["invc"], writes=["small16"])
                S.add("dve", lambda e, p3=p3, cc=cc: e.tensor_tensor(
                    out=dbf[:, cc, 0:16], in0=small[:, 0:16], in1=p3[:, 0, 15:31], op=ALU.subtract),
                    reads=["small16", "pext"], writes=[("dbf", cc)])
            tgb = tmp["tgb"][0][:, 0:N]
            S.add("act", lambda e, tgb=tgb: e.activation(out=tgb, in_=ps[BGB][:, 0:N], func=AF.Tanh, scale=0.5),
                  reads=[PS(BGB)], writes=["tgb"])
            S.add("dve", lambda e, tgb=tgb, cc=cc: e.scalar_tensor_tensor(
                out=sgb[:, cc, 0:N], in0=tgb, scalar=1.0, in1=ps[BGB][:, 0:N], op0=ALU.add, op1=ALU.mult),
                reads=["tgb", PS(BGB)], writes=[("sgb", cc)])
            if c > 0:
                tail(c - 1)
            if c == 2 and mid_hook is not None:
                mid_hook()
        c0_s3, c0_k3 = ring_load(win_s[l, 64])
        c0_s4, c0_k4 = ring_load(win_s[l, 80])
        proj(2, c0_s3, c0_k3, hT, hkeys)
        proj(3, c0_s4, c0_k4, hT, hkeys)
        pool_mm(*pend_pw)
        gates(15)
        tail(15)
        if DEBUG_STOP == 22:
            raise StopBuild()
        yakeys = [("ya", k) for k in range(16)]
        ybkeys = [("yb", k) for k in range(16)]
        for c in range(16):
            base = 0 if c % 2 == 0 else 4
            if c > 0:
                s3, k3 = ring_load(win_s[l, 64 + c])
                s4, k4 = ring_load(win_s[l, 80 + c])
            s2, k2 = ring_load(pb_s[l, c])
            s1, k1 = ring_load(pa_s[l, c])
            if c > 0:
                proj(base + 2, s3, k3, hT, hkeys)
                proj(base + 3, s4, k4, hT, hkeys)
            proj(base + 1, s2, k2, yb, ybkeys)
            proj(base + 0, s1, k1, ya, yakeys)
            q = 0
            t1 = tmp["tr"][q][:, 0:N]
            t2 = tmp["ti"][q][:, 0:N]
            S.add("act", lambda e, t1=t1, base=base: e.activation(out=t1, in_=ps[base + 2][:, 0:N], func=AF.Tanh,
                                                                  scale=0.5),
                  reads=[PS(base + 2)], writes=["tr"])
            S.add("act", lambda e, t2=t2, base=base: e.activation(out=t2, in_=ps[base + 3][:, 0:N], func=AF.Tanh,
                                                                  scale=0.5),
                  reads=[PS(base + 3)], writes=["ti"])
            S.add("dve", lambda e, t1=t1, base=base: e.scalar_tensor_tensor(
                out=t1, in0=t1, scalar=1.0, in1=ps[base + 0][:, 0:N], op0=ALU.add, op1=ALU.mult),
                reads=["tr", PS(base + 0)], writes=["tr"])
            S.add("dve", lambda e, t2=t2, base=base: e.scalar_tensor_tensor(
                out=t2, in0=t2, scalar=1.0, in1=ps[base + 1][:, 0:N], op0=ALU.add, op1=ALU.mult),
                reads=["ti", PS(base + 1)], writes=["ti"])
            S.add("dve", lambda e, t1=t1, t2=t2, c=c: e.tensor_tensor(out=mT[:, c, 0:N], in0=t1, in1=t2,
                                                                      op=ALU.add),
                  reads=["tr", "ti"], writes=[("mT", c)])
        if DEBUG_STOP == 23:
            raise StopBuild()
        mkeys = [("mT", k) for k in range(16)]
        bi_ = 0
        for gi, cfg in enumerate(gcfgs):
            S.add("pool", lambda e, gi=gi, cfg=cfg: e.dma_start(out=gtiles[gi], in_=gbc_s[l, cfg]),
                  writes=gkeys[gi], dma=("gbc", gi))
        for nb in range(4):
            wslot, wkeys = ring_load(wo_s[l, nb], nslots=4)
            if prefetch is not None and nb == 1:
                src_rows, nsub_n = prefetch
                for j in range(min(2, nsub_n)):
                    dma("sp", "xin", xstage[0][:, j * D:(j + 1) * D],
                        src_rows[j * 128:(j + 1) * 128, :], writes=xstage_keys[0])
            wv = ring[:, wslot:wslot + 4, :].rearrange("p a b -> p (a b)").rearrange("p (k n) -> p k n", k=16)
            for j in range(nsub):
                bank = bi_ % 8
                bi_ += 1
                for k in range(16):
                    S.add("pe", lambda e, bank=bank, j=j, k=k, wv=wv: e.matmul(
                        ps[bank][:, :], lhsT=mT[:, k, j * 128:(j + 1) * 128], rhs=wv[:, k, :],
                        start=(k == 0), stop=(k == 15)), reads=wkeys + [mkeys[k]], writes=[PS(bank)])
                gi = 0 if len(gcfgs) == 1 else j
                rs = tmp["rs"][0]
                S.add("dve", lambda e, bank=bank, rs=rs, gi=gi, nb=nb: e.tensor_tensor(
                    out=rs[:], in0=ps[bank][:, :], in1=gtiles[gi][:, nb * 512:(nb + 1) * 512], op=ALU.mult),
                    reads=[PS(bank)] + gkeys[gi], writes=[("rs", 0)])
                S.add("dve", lambda e, rs=rs, j=j, nb=nb: e.tensor_tensor(
                    out=xt[:, j, nb * 512:(nb + 1) * 512], in0=xt[:, j, nb * 512:(nb + 1) * 512], in1=rs[:],
                    op=ALU.add), reads=[("rs", 0), ("xt", j)], writes=[("xt", j)])
        if prefetch is not None and prefetch[1] > 2:
            src_rows, nsub_n = prefetch
            for j in range(2, nsub_n):
                dma("sp", "xin", xstage[1][:, (j - 2) * D:(j - 1) * D],
                    src_rows[j * 128:(j + 1) * 128, :], writes=xstage_keys[1])

    eps_t = sb("eps_t", [128, 1])
    S.add("dve", lambda e: e.memset(eps_t[:], EPS), writes=["eps"])
    S.add("dve", lambda e: e.memset(q25[:], 0.25), writes=["q25"])

    def final_norm(nsub, dst_rows):
        for j in range(nsub):
            S.add("act", lambda e, j=j: e.activation(out=xn[j % 2][:], in_=xt[:, j, :], func=AF.Square,
                                                      accum_out=ss[:, j:j + 1]),
                  reads=[("xt", j)], writes=[("xn", 0), ("ss", j)])
        S.add("act", lambda e: e.activation(out=sd[:, 0:nsub], in_=ss[:, 0:nsub], func=AF.Sqrt,
                                            scale=1.0 / D, bias=eps_t[:, 0:1]),
              reads=[("ss", j) for j in range(nsub)] + ["eps"], writes=["sd"])
        S.add("dve", lambda e: e.reciprocal(out=rstd[:, 0:nsub], in_=sd[:, 0:nsub]), reads=["sd"], writes=["rstd"])
        S.add("pool", lambda e: e.dma_start(out=bc[:], in_=fgb_d), writes=["bc"], dma=("gbc", 0))
        for j in range(nsub):
            S.add("dve", lambda e, j=j: e.scalar_tensor_tensor(
                out=xt[:, j, :], in0=xt[:, j, :], scalar=rstd[:, j:j + 1], in1=bc[:], op0=ALU.mult, op1=ALU.mult),
                reads=[("xt", j), "rstd", "bc"], writes=[("xt", j)])
            S.add("pool", lambda e, j=j: e.dma_start(out=dst_rows[j * 128:(j + 1) * 128, :], in_=xt[:, j, :]),
                  reads=[("xt", j)], dma="yout")

    try:
        if DEBUG_STOP >= 20:
            dma("sp", "xin", xt[:, 0:4, :], xp[0:512, :].rearrange("(j p) d -> p j d", p=128),
                writes=[("xt", j) for j in range(4)])
            layer(0, 512, 1, [0], [0], True)
    except StopBuild:
        S.final_wait()
        S.emit(nc, es)
        es.close()
        return nc
    def x_from_stage(nsub_n):
        for j in range(nsub_n):
            S.add("pool", lambda e, j=j: e.tensor_copy(out=xt[:, j, :],
                                                       in_=xstage[j // 2][:, (j % 2) * D:(j % 2 + 1) * D]),
                  reads=xstage_keys[j // 2], writes=[("xt", j)])

    def boundary(tprev):
        return lambda: final_norm(4, yp[tprev * 512:(tprev + 1) * 512, :])

    for t in range(NT_P):
        if t == 0:
            dma("sp", "xin", xt[:, 0:4, :], xp[0:512, :].rearrange("(j p) d -> p j d", p=128),
                writes=[("xt", j) for j in range(4)])
        nxt = (xp[(t + 1) * 512:(t + 2) * 512, :], 4) if t + 1 < NT_P else (xs, 2)
        for l in range(2):
            layer(l, 512, 1, [0], [0], t == 0, prefetch=(nxt if l == 1 else None),
                  xsrc=(l == 0 and t > 0), after_norm=(boundary(t - 1) if (l == 0 and t > 0) else None),
                  mid_hook=((lambda: x_from_stage(4)) if (l == 0 and t > 0) else None))
            if DEBUG_STOP == 1:
                break
        if DEBUG_STOP == 1:
            S.final_wait()
            S.emit(nc, es)
            es.close()
            return nc
    for l in range(2):
        layer(l, 256, 4, [1, 2, 3, 4], [1, 2], False, xsrc=(l == 0),
              after_norm=(boundary(NT_P - 1) if l == 0 else None),
              mid_hook=((lambda: x_from_stage(2)) if l == 0 else None))
    final_norm(2, ys)
    S.add("pool", lambda e: e.dma_start(out=oconv_d, in_=Sconv[:].rearrange("p a b c d -> p (a b c d)")),
          reads=[("Sconv", l_, c_) for l_ in range(2) for c_ in range(16)], dma="sout")
    S.add("pool", lambda e: e.dma_start(out=opool_d, in_=Spool[:].rearrange("p a b c d -> p (a b c d)")),
          reads=[("Spool", l_, c_) for l_ in range(2) for c_ in range(16)], dma="sout")
    S.add("pool", lambda e: e.dma_start(out=olru_d, in_=Slru[:].rearrange("p a b c -> p (a b c)")),
          reads=[("Slru", l_, c_) for l_ in range(2) for c_ in range(16)], dma="sout")
    S.final_wait()
    S.emit(nc, es)
    es.close()
    return nc


_NC = None


def _get_nc():
    global _NC
    if _NC is None:
        _NC = build_program()
    return _NC


def _fm(v):
    v = np.asarray(v, np.float32)
    lead = v.shape[:-1]
    r = v.reshape(*lead, NCH, 128)
    return np.ascontiguousarray(np.moveaxis(r, -1, 0))


def kernel(x_prompt, x_sample, c_prompt, c_sample, state_conv, state_lru, state_pool,
           norm_g, w_ada, b_ada, w_in, conv_w, conv_b, lru_wa, lru_ba, lru_wx, lru_bx,
           lru_lam, pool_w, pool_scale, w_proj_a, w_proj_b, w_out, final_g):
    f32 = np.float32
    x_prompt = np.asarray(x_prompt, f32)
    x_sample = np.asarray(x_sample, f32)
    nc = _get_nc()
    vecs = np.zeros((128, 2, 10, NCH), f32)
    for l in range(2):
        rows = [norm_g[l], conv_w[l][0], conv_w[l][1], conv_w[l][2], conv_w[l][3], conv_b[l],
                lru_ba[l], lru_bx[l], lru_lam[l], pool_scale[l]]
        for v, r in enumerate(rows):
            vecs[:, l, v, :] = np.asarray(r, f32).reshape(NCH, 128).T
    b_ada = np.asarray(b_ada, f32)
    badaT = np.ascontiguousarray(b_ada[:, :4096].reshape(2, 32, 128).transpose(2, 0, 1))
    bgate = np.ascontiguousarray(np.broadcast_to(b_ada[None, :, 4096:], (5, 2, D)))
    fgb = np.ascontiguousarray(np.broadcast_to(np.asarray(final_g, f32)[None, :], (128, D)))
    sel = np.zeros((5, 3, 128), f32)
    sel[0, 0, :] = 1.0
    sel[1, 1, :64] = 1.0
    sel[2, 1, 64:] = 1.0
    sel[3, 2, :64] = 1.0
    sel[4, 2, 64:] = 1.0
    invc = np.zeros((128, 4, 16), f32)
    for g, w in enumerate((2, 4, 8, 16)):
        invc[:, g, :] = 1.0 / np.minimum(w, np.arange(16) + 1).astype(f32)
    ident = np.eye(128, dtype=f32)
    shared = {
        "vecs": vecs.reshape(128, -1), "badaT": badaT.reshape(128, -1), "bgate": bgate.reshape(5, -1),
        "fgb": fgb, "sel": sel.reshape(5, -1), "invc": invc.reshape(128, -1), "ident": ident,
        "w_ada": np.asarray(w_ada, f32), "w_in": np.asarray(w_in, f32),
        "lru_wa": np.asarray(lru_wa, f32).reshape(2, D, 128), "lru_wx": np.asarray(lru_wx, f32).reshape(2, D, 128),
        "pool_w": np.asarray(pool_w, f32).reshape(2, D, 512),
        "w_proj_a": np.asarray(w_proj_a, f32), "w_proj_b": np.asarray(w_proj_b, f32), "w_out": np.asarray(w_out, f32),
    }
    zeros_seq = np.zeros((SEQ, D), f32)
    state_conv = np.asarray(state_conv, f32)
    state_pool = np.asarray(state_pool, f32)
    state_lru = np.asarray(state_lru, f32)
    in_maps = []
    for k in range(NCORES):
        m = dict(shared)
        m["xp"] = x_prompt[k] if k < 2 else zeros_seq
        m["xs"] = np.ascontiguousarray(x_sample[4 * k:4 * k + 4].reshape(256, D))
        c5 = np.concatenate([np.asarray(c_prompt, f32)[k % 2][None], np.asarray(c_sample, f32)[4 * k:4 * k + 4]], 0)
        m["c5T"] = np.ascontiguousarray(c5.reshape(5, NCH, 128).transpose(2, 1, 0)).reshape(128, -1)
        sc = state_conv[:, 4 * k:4 * k + 4]
        m["sconvT"] = np.ascontiguousarray(sc.reshape(2, 4, 3, NCH, 128).transpose(4, 0, 1, 3, 2)).reshape(128, -1)
        sp_ = state_pool[:, 4 * k:4 * k + 4]
        m["spoolT"] = np.ascontiguousarray(sp_.reshape(2, 4, 15, NCH, 128).transpose(4, 0, 1, 3, 2)).reshape(128, -1)
        sl = state_lru[:, 4 * k:4 * k + 4]
        m["slruT"] = np.ascontiguousarray(sl.reshape(2, 4, NCH, 128).transpose(3, 0, 1, 2)).reshape(128, -1)
        in_maps.append(m)
    res = run_bass_kernel_spmd(nc, in_maps, core_ids=list(range(NCORES)))
    R = res.results
    y_prompt = np.stack([np.asarray(R[k]["yp"], f32) for k in range(2)], 0)
    y_sample = np.concatenate([np.asarray(R[k]["ys"], f32).reshape(4, 64, D) for k in range(NCORES)], 0)

    def unfm(a, tail):
        a = np.asarray(a, f32).reshape(128, 2, 5, NCH, *tail)
        if tail:
            a = a.transpose(1, 2, 4, 3, 0)
        else:
            a = a.transpose(1, 2, 3, 0)
        return np.ascontiguousarray(a).reshape(2, 5, *tail, D)

    oc = [unfm(R[k]["o_conv"], (3,)) for k in range(NCORES)]
    op = [unfm(R[k]["o_pool"], (15,)) for k in range(NCORES)]
    ol = [unfm(R[k]["o_lru"], ()) for k in range(NCORES)]
    conv_p = np.stack([oc[k][:, 0] for k in range(2)], 1)
    lru_p = np.stack([ol[k][:, 0] for k in range(2)], 1)
    pool_p = np.stack([op[k][:, 0] for k in range(2)], 1)
    conv_s = np.concatenate([oc[k][:, 1:5] for k in range(NCORES)], 1)
    lru_s = np.concatenate([ol[k][:, 1:5] for k in range(NCORES)], 1)
    pool_s = np.concatenate([op[k][:, 1:5] for k in range(NCORES)], 1)
    return (y_prompt, y_sample, conv_p, lru_p, pool_p, conv_s, lru_s, pool_s)
```
